# Optimizing a Trainium2 kernel written in Bass

```python
import math
import jax, jax.numpy as jnp
from jax import lax
import numpy as np

D_MODEL = 1024
BATCH = 8
SEQ = 2048
DEPTH = 2
DEC_BATCH = 128
DEC_SEQ = 8
PAST_LEN = 16384
PAGE_SIZE = 128

EPS = 1e-6
N_EVEN = (DEPTH + 1) // 2
N_ODD = DEPTH // 2
D_POOL = D_MODEL // 2
POOL_WINDOWS = (2, 4, 8, 16)
POOL_GROUPS = len(POOL_WINDOWS)
POOL_GC = D_POOL // POOL_GROUPS
POOL_PREV = max(POOL_WINDOWS) - 1
D_CONV = D_MODEL // 2
CONV_W = 3
D_AB_IN = D_POOL + 3 * D_CONV
D_SGU = D_MODEL
SGU_HEADS = 8
SGU_HD = D_SGU // SGU_HEADS
CHUNK = 128
D_FF = 4 * D_MODEL

kernel_name = "hybrid_pool_conv_sgu_decoder_step"


def rmsnorm(x, g):
    xf = x.astype(jnp.float32)
    y = xf * lax.rsqrt(jnp.mean(xf * xf, axis=-1, keepdims=True) + EPS) * g.astype(jnp.float32)
    return y.astype(x.dtype)


def pool_mixer(u, prev, start_pos, w_grp, scale):
    b, s, _ = u.shape
    full = jnp.concatenate([prev.astype(u.dtype), u], axis=1)
    cs = jnp.cumsum(full.astype(jnp.float32), axis=1)
    cs = jnp.concatenate([jnp.zeros_like(cs[:, :1]), cs], axis=1)
    hi = cs[:, POOL_PREV + 1:]
    pos = (jnp.arange(s) + start_pos).astype(jnp.float32)
    outs = []
    for g, w in enumerate(POOL_WINDOWS):
        sl = slice(g * POOL_GC, (g + 1) * POOL_GC)
        lo = cs[:, POOL_PREV + 1 - w:POOL_PREV + 1 - w + s, sl]
        cnt = jnp.minimum(pos + 1.0, float(w))[None, :, None]
        outs.append((hi[..., sl] - lo) / cnt - u[..., sl].astype(jnp.float32))
    d = jnp.stack(outs, axis=2).astype(u.dtype)
    y = jnp.einsum('bsgc,gcd->bsgd', d, w_grp).reshape(b, s, D_POOL) * scale
    return y, full[:, -POOL_PREV:]


def short_conv(z, prev, w):
    s = z.shape[1]
    full = jnp.concatenate([prev.astype(z.dtype), z], axis=1)
    out = w[0] * full[:, :s]
    for k in range(1, CONV_W):
        out = out + w[k] * full[:, k:k + s]
    return out, full[:, -(CONV_W - 1):]


def pool_conv_mixer(h, pool_prev, conv_prev, start_pos, w_in, w_grp, scale, conv_w, w_out):
    p = h @ w_in
    u = p[..., :D_POOL]
    xb = p[..., D_POOL:D_POOL + D_CONV]
    gate_b = p[..., D_POOL + D_CONV:D_POOL + 2 * D_CONV]
    gate_c = p[..., D_POOL + 2 * D_CONV:]
    ya, new_pool = pool_mixer(u, pool_prev, start_pos, w_grp, scale)
    cz, new_conv = short_conv(gate_c * xb, conv_prev, conv_w)
    yb = gate_b * cz
    return jnp.concatenate([ya, yb], axis=-1) @ w_out, new_pool, new_conv


def chunk_spatial(v, w_s, b_s):
    b, s, h, hd = v.shape
    mask = jnp.tril(jnp.ones((CHUNK, CHUNK), dtype=w_s.dtype))
    wm = w_s * mask
    if s < CHUNK:
        out = jnp.einsum('hts,bshd->bthd', wm[:, :s, :s], v)
        return out + jnp.transpose(b_s[:, :s])[None, :, :, None]
    n = -(-s // CHUNK)
    vp = jnp.pad(v, ((0, 0), (0, n * CHUNK - s), (0, 0), (0, 0))).reshape(b, n, CHUNK, h, hd)
    out = jnp.einsum('hts,bnshd->bnthd', wm, vp) + jnp.transpose(b_s)[None, None, :, :, None]
    return out.reshape(b, n * CHUNK, h, hd)[:, :s]


def chunk_mlp_mixer(h, w_uv, g_v, w_s, b_s, w_out):
    b, s, _ = h.shape
    p = h @ w_uv
    u = p[..., :D_SGU]
    v = rmsnorm(p[..., D_SGU:], g_v)
    sv = chunk_spatial(v.reshape(b, s, SGU_HEADS, SGU_HD), w_s, b_s).reshape(b, s, D_SGU)
    return (u * sv) @ w_out, v


def trunk(x, pool_prev, conv_prev, start_pos,
          g_mix_pre, g_mix_post, g_ffn_pre, g_ffn_post,
          w_in_ab, w_pool_grp, pool_scale, conv_w, w_out_ab,
          w_uv, g_v, w_spatial, b_spatial, w_out_c, w_up, w_down):
    new_pool, new_conv, new_v = [], [], []
    for l in range(DEPTH):
        h = rmsnorm(x, g_mix_pre[l])
        if l % 2 == 0:
            i = l // 2
            m, npool, nconv = pool_conv_mixer(h, pool_prev[i], conv_prev[i], start_pos,
                                              w_in_ab[i], w_pool_grp[i], pool_scale[i],
                                              conv_w[i], w_out_ab[i])
            new_pool.append(npool)
            new_conv.append(nconv)
        else:
            i = l // 2
            m, v = chunk_mlp_mixer(h, w_uv[i], g_v[i], w_spatial[i], b_spatial[i], w_out_c[i])
            new_v.append(v)
        x = x + rmsnorm(m, g_mix_post[l])
        h = rmsnorm(x, g_ffn_pre[l])
        f = jnp.square(jax.nn.relu(h @ w_up[l])) @ w_down[l]
        x = x + rmsnorm(f, g_ffn_post[l])
    return x, jnp.stack(new_pool), jnp.stack(new_conv), jnp.stack(new_v)


def setup_inputs(seed: int = 0) -> dict:
    key = jax.random.key(seed)
    ks = jax.random.split(key, 24)
    nrm = lambda k, shape, sc: jax.random.normal(k, shape, jnp.float32) * sc
    gain = lambda k, shape: 1.0 + 0.05 * jax.random.normal(k, shape, jnp.float32)
    return {
        "x_prompt": nrm(ks[0], (BATCH, SEQ, D_MODEL), 1.0),
        "x_sample": nrm(ks[1], (DEC_BATCH, DEC_SEQ, D_MODEL), 1.0),
        "state_pool": nrm(ks[2], (N_EVEN, DEC_BATCH, POOL_PREV, D_POOL), 1.0),
        "state_conv": nrm(ks[3], (N_EVEN, DEC_BATCH, CONV_W - 1, D_CONV), 1.0),
        "g_mix_pre": gain(ks[4], (DEPTH, D_MODEL)),
        "g_mix_post": gain(ks[5], (DEPTH, D_MODEL)),
        "g_ffn_pre": gain(ks[6], (DEPTH, D_MODEL)),
        "g_ffn_post": gain(ks[7], (DEPTH, D_MODEL)),
        "w_in_ab": nrm(ks[8], (N_EVEN, D_MODEL, D_AB_IN), D_MODEL ** -0.5),
        "w_pool_grp": nrm(ks[9], (N_EVEN, POOL_GROUPS, POOL_GC, POOL_GC), POOL_GC ** -0.5),
        "pool_scale": gain(ks[10], (N_EVEN, D_POOL)),
        "conv_w": nrm(ks[11], (N_EVEN, CONV_W, D_CONV), CONV_W ** -0.5),
        "w_out_ab": nrm(ks[12], (N_EVEN, D_POOL + D_CONV, D_MODEL), (D_POOL + D_CONV) ** -0.5),
        "w_uv": nrm(ks[13], (N_ODD, D_MODEL, 2 * D_SGU), D_MODEL ** -0.5),
        "g_v": gain(ks[14], (N_ODD, D_SGU)),
        "w_spatial": nrm(ks[15], (N_ODD, SGU_HEADS, CHUNK, CHUNK), CHUNK ** -0.5),
        "b_spatial": gain(ks[16], (N_ODD, SGU_HEADS, CHUNK)),
        "w_out_c": nrm(ks[17], (N_ODD, D_SGU, D_MODEL), D_SGU ** -0.5),
        "w_up": nrm(ks[18], (DEPTH, D_MODEL, D_FF), D_MODEL ** -0.5),
        "w_down": nrm(ks[19], (DEPTH, D_FF, D_MODEL), D_FF ** -0.5),
    }


def reference(x_prompt, x_sample, state_pool, state_conv,
              g_mix_pre, g_mix_post, g_ffn_pre, g_ffn_post,
              w_in_ab, w_pool_grp, pool_scale, conv_w, w_out_ab,
              w_uv, g_v, w_spatial, b_spatial, w_out_c, w_up, w_down):
    b = x_prompt.shape[0]
    zero_pool = jnp.zeros((N_EVEN, b, POOL_PREV, D_POOL), x_prompt.dtype)
    zero_conv = jnp.zeros((N_EVEN, b, CONV_W - 1, D_CONV), x_prompt.dtype)
    y_prompt, pool_p, conv_p, _ = trunk(
        x_prompt, zero_pool, zero_conv, 0,
        g_mix_pre, g_mix_post, g_ffn_pre, g_ffn_post,
        w_in_ab, w_pool_grp, pool_scale, conv_w, w_out_ab,
        w_uv, g_v, w_spatial, b_spatial, w_out_c, w_up, w_down)
    y_sample, pool_s, conv_s, v_s = trunk(
        x_sample, state_pool, state_conv, PAST_LEN,
        g_mix_pre, g_mix_post, g_ffn_pre, g_ffn_post,
        w_in_ab, w_pool_grp, pool_scale, conv_w, w_out_ab,
        w_uv, g_v, w_spatial, b_spatial, w_out_c, w_up, w_down)
    return (y_prompt, y_sample, pool_p, pool_s, conv_p, conv_s, v_s)
```

```python
from contextlib import ExitStack

import numpy as np
import concourse.bass as bass
import concourse.mybir as mybir
from concourse.bass_utils import run_bass_kernel_spmd

F32 = mybir.dt.float32
BF16 = mybir.dt.bfloat16
I32 = mybir.dt.int32
ALU = mybir.AluOpType
AF = mybir.ActivationFunctionType

NCORES = 8
D = 1024
KC = 8
SEQ = 2048
DEC_B = 16
DEC_T = 8
EPS = 1e-6
TM = 1152
WINS = (2, 4, 8, 16)
COMPUTE = ("pe", "act", "dve", "pool")
SELF_SYNC = True
FAST_RECIP = False


class Res:
    __slots__ = ("name", "last_w", "readers", "dma_count", "excl")

    def __init__(self, name, excl=False):
        self.name = name
        self.excl = excl
        self.last_w = None
        self.readers = []
        self.dma_count = 0


class Plan:
    def __init__(self, nc, es):
        self.nc = nc
        self.es = es
        self.ops = {e: [] for e in COMPUTE + ("sp",)}
        self.count = {e: 0 for e in COMPUTE}
        self.seen = {e: {} for e in COMPUTE + ("sp",)}
        self.pending = {e: [] for e in COMPUTE + ("sp",)}
        self.sems = {}
        self.phase = "setup"
        self.pe_log = []

    def sem(self, key):
        if key not in self.sems:
            nm = "s_" + (key if isinstance(key, str) else "d_" + key[1])
            self.sems[key] = self.es.enter_context(self.nc.semaphore(nm))
        return self.sems[key]

    def emit(self, eng, fn, reads=(), writes=(), dma_res=None, n_dma=1):
        deps = []
        for r in reads:
            if r.last_w is not None:
                deps.append(r.last_w)
            if r.excl:
                deps.extend(t for t in r.readers if t[0] != eng)
        for w in writes:
            if w.last_w is not None:
                deps.append(w.last_w)
            deps.extend(w.readers)
        deps.extend(self.pending[eng])
        self.pending[eng] = []
        need = {}
        for k, v in deps:
            if k == eng and (eng == "pe" or not SELF_SYNC):
                continue
            if v > need.get(k, 0):
                need[k] = v
        waits = []
        seen = self.seen[eng]
        for k, v in need.items():
            if seen.get(k, 0) < v:
                seen[k] = v
                waits.append((self.sem(k), v))
        inc = None
        ticket = None
        if fn is not None:
            if dma_res is not None:
                key = ("dma", dma_res.name)
                dma_res.dma_count += 16 * n_dma
                ticket = (key, dma_res.dma_count)
                inc = (self.sem(key), 16)
            else:
                self.count[eng] += 1
                ticket = (eng, self.count[eng])
                inc = (self.sem(eng), 1)
        self.ops[eng].append((waits, fn, inc))
        if eng == "pe" and fn is not None:
            self.pe_log.append([self.phase, getattr(fn, "n_mm", 1)])
        if ticket is not None:
            for r in reads:
                r.readers.append(ticket)
            for w in writes:
                w.last_w = ticket
                w.readers = []
        return ticket

    def barrier(self):
        for e in COMPUTE:
            if e == "pe":
                continue
            for f in COMPUTE:
                if self.count[f] > 0 and (f != e or SELF_SYNC):
                    self.pending[e].append((f, self.count[f]))

    def replay(self, eng_obj, name):
        for waits, fn, inc in self.ops[name]:
            for sem, v in waits:
                eng_obj.wait_ge(sem, v)
            if fn is None:
                continue
            r = fn(eng_obj)
            if inc is not None:
                sem, amt = inc
                if isinstance(r, (list, tuple)):
                    for i in r:
                        i.then_inc(sem, amt)
                else:
                    r.then_inc(sem, amt)


def build_program(debug=False):
    nc = bass.Bass("TRN2", target_bir_lowering=False)
    dbg = nc.dram_tensor("dbg", [8, 128, KC * TM], F32, kind="ExternalOutput").ap() if debug else None

    def din(name, shape):
        return nc.dram_tensor(name, list(shape), F32, kind="ExternalInput").ap()

    def dout(name, shape):
        return nc.dram_tensor(name, list(shape), F32, kind="ExternalOutput").ap()

    xp = din("xp", [SEQ, D])
    xs = din("xs", [128, D])
    sp_in = din("sp", [240, 512])
    sc_in = din("sc", [32, 512])
    g_mix_pre = din("g_mix_pre", [2, D])
    g_mix_post = din("g_mix_post", [2, D])
    g_ffn_pre = din("g_ffn_pre", [2, D])
    g_ffn_post = din("g_ffn_post", [2, D])
    w_in_ab = din("w_in_ab", [D, 2048])
    w_pool_grp = din("w_pool_grp", [4, 128, 128])
    pool_scale = din("pool_scale", [1, 512])
    conv_w = din("conv_w", [3, 512])
    w_out_ab = din("w_out_ab", [D, D])
    w_uv = din("w_uv", [D, 2048])
    g_v = din("g_v", [1, D])
    w_spatial = din("w_spatial", [8, 128, 128])
    b_spatial = din("b_spatial", [8, 128])
    w_out_c = din("w_out_c", [D, D])
    w_up = din("w_up", [2, D, 4096])
    w_down = din("w_down", [2, 4096, D])

    yp = dout("yp", [SEQ, D])
    ys = dout("ys", [128, D])
    o_pool_p = dout("o_pool_p", [15, 512])
    o_pool_s = dout("o_pool_s", [240, 512])
    o_conv_p = dout("o_conv_p", [2, 512])
    o_conv_s = dout("o_conv_s", [32, 512])
    o_v_s = dout("o_v_s", [128, D])

    with ExitStack() as es:
        P = Plan(nc, es)

        def sb(name, shape, dt=F32):
            return es.enter_context(nc.sbuf_tensor(name, list(shape), dt))

        Xt = sb("X", [128, KC * TM])
        Xv = Xt[:, :].rearrange("p (k t) -> p k t", t=TM)
        arena = sb("arena", [128, 23040])
        WSTt = [sb(f"wst{i}", [128, 2048]) for i in range(2)]
        WBFt = [sb(f"wbf{i}", [128, 2048], BF16) for i in range(3)]
        XTt = [sb(f"xt{i}", [128, 1024]) for i in range(3)]
        RAt = sb("ra", [128, TM])
        RBt = sb("rb", [128, TM])
        SQt = [sb(f"sq{i}", [128, 512], BF16) for i in range(4)]
        RTt = [sb(f"rt{i}", [128, 512]) for i in range(2)]
        ident = sb("ident", [128, 128])
        maskt = sb("mask", [128, 128])
        onesf = sb("onesf", [128, 128])
        onesb = sb("onesb", [128, 128], BF16)
        GT = sb("gt", [128, KC * 16])
        GTv = GT[:, :].rearrange("p (k r) -> p k r", r=16)
        GTI = sb("gti", [128, KC * 16])
        GTIv = GTI[:, :].rearrange("p (k r) -> p k r", r=16)
        WGt = sb("wg", [128, 512], BF16)
        WGv = WGt[:, :].rearrange("p (g d) -> p g d", d=128)
        WMTP = sb("wmtp", [128, 1024], BF16)
        WMTS = sb("wmts", [128, 1024], BF16)
        BIASP = sb("biasp", [128, 1024])
        GVBC = sb("gvbc", [128, 1024])
        INVC = sb("invc", [128, 64])
        INVCi = sb("invci", [128, 16], I32)
        SPT = sb("spt", [128, 4 * 240])
        SCT = sb("sct", [128, 4 * 32])
        HALOP = sb("halop", [128, 4 * 15])
        HALOC = sb("haloc", [128, 4 * 2])
        TMP16 = sb("tmp16", [128, 16])
        SS = sb("ss", [128, 16])
        RV = sb("rv", [128, 16])
        PSt = [es.enter_context(nc.psum_tensor(f"ps{i}", [128, 512], F32)) for i in range(8)]

        Hv = arena[:, 0:4608].bitcast(BF16).rearrange("p (k t) -> p k t", t=TM)
        Mv = arena[:, 4608:13824].rearrange("p (k t) -> p k t", t=TM)
        HIDv = arena[:, 13824:23040].bitcast(BF16).rearrange("p (k t) -> p k t", t=TM)
        Yv = arena[:, 13824:18432].bitcast(BF16).rearrange("p (k t) -> p k t", t=TM)
        Gv = arena[:, 18432:23040].bitcast(BF16).rearrange("p (k t) -> p k t", t=TM)
        VNv = arena[:, 13824:18432].bitcast(BF16).rearrange("p (n d) -> p n d", d=1024)
        Vv = arena[:, 4608:13824].rearrange("p (n d) -> p n d", d=1024)
        CZv = arena[:, 18432:23040].rearrange("p (k t) -> p k t", t=TM)
        Ub = [arena[:, 4608:6016], arena[:, 6016:7424]]
        S1b = arena[:, 7424:8832]
        S2b = arena[:, 8832:10240]
        Db = [arena[:, 10240:10816].bitcast(BF16), arena[:, 10816:11392].bitcast(BF16)]
        Zb = [arena[:, 11392:12580], arena[:, 12580:13768]]
        S3b = arena[:, 18432 + 2 * TM: 18432 + 2 * TM + 1408]

        NS = 3
        rX = [[Res(f"X{k}_{s}") for s in range(NS)] for k in range(KC)]
        rH = [[Res(f"H{k}_{s}") for s in range(NS)] for k in range(KC)]
        rM = [[Res(f"M{k}_{s}") for s in range(NS)] for k in range(KC)]
        rY = [[Res(f"Y{k}_{s}") for s in range(NS)] for k in range(KC)]
        rG = [[Res(f"G{k}_{s}") for s in range(NS)] for k in range(KC)]
        rSVB = [[Res(f"SVB{k}_{s}") for s in range(NS)] for k in range(KC)]
        rHID = [[Res(f"HID{k}_{s}") for s in range(NS)] for k in range(16)]
        rRA = [Res(f"RA{s}") for s in range(NS)]
        rRB = [Res(f"RB{s}") for s in range(NS)]
        rPS = [Res(f"PS{i}", excl=True) for i in range(8)]
        rWST = [Res(f"WST{i}") for i in range(2)]
        rWBF = [Res(f"WBF{i}") for i in range(3)]
        rWBFb = [Res(f"WBFb{i}") for i in range(3)]
        rXT = [Res(f"XT{i}") for i in range(3)]
        rSQ = [Res(f"SQ{i}") for i in range(4)]
        rRT = [Res(f"RT{i}") for i in range(2)]
        rU = [Res("U0"), Res("U1")]
        rS1, rS2 = Res("S1"), Res("S2")
        rD = [Res("D0"), Res("D1")]
        rZ = [Res("Z0"), Res("Z1")]
        rCZ = [Res(f"CZ{i}") for i in range(4)]
        rV = [Res(f"V{n}") for n in range(9)]
        rVN = [Res(f"VN{n}") for n in range(9)]
        rC = Res("consts")
        rID = Res("ident")
        rINV = Res("invc")
        rWG = Res("wg")
        rBIAS = Res("bias")
        rWMT = Res("wmt")
        rSPT = [Res(f"SPT{g}") for g in range(4)]
        rSCT = [Res(f"SCT{g}") for g in range(4)]
        rHALOP = [Res(f"HALOP{g}") for g in range(4)]
        rHALOC = [Res(f"HALOC{g}") for g in range(4)]
        rTMP16 = Res("TMP16")
        rSS = Res("SS")
        rSSn = [Res(f"SS{n}") for n in range(9)]

        state = {"bank": 0, "sq": 0, "rt": 0, "xt": 0}

        def next_bank():
            banks = state.get("banks", (0, 1, 2, 3, 4))
            i = state["bank"]
            state["bank"] = i + 1
            return banks[i % len(banks)]

        def next_sq():
            i = state["sq"]
            state["sq"] = (i + 1) % 4
            return i

        def next_rt():
            i = state["rt"]
            state["rt"] = (i + 1) % 2
            return i

        def next_xt():
            i = state["xt"]
            state["xt"] = (i + 1) % 3
            return i

        STAT = [5, 6, 7]
        deferred = []

        def flush_deferred(keep=0):
            nflush = len(deferred) - keep
            if nflush <= 0:
                return
            items = list(deferred[:nflush])
            del deferred[:nflush]
            for f in items:
                f()

        def op_act(out, in_, func, reads, writes, **kw):
            P.emit("act", lambda e: e.activation(out=out, in_=in_, func=func, **kw), reads, writes)

        def op_tt(eng, out, in0, in1, op, reads, writes):
            P.emit(eng, lambda e: e.tensor_tensor(out=out, in0=in0, in1=in1, op=op), reads, writes)

        def op_stt(out, in0, scalar, in1, op0, op1, reads, writes):
            P.emit("dve", lambda e: e.scalar_tensor_tensor(out=out, in0=in0, scalar=scalar, in1=in1, op0=op0, op1=op1),
                   reads, writes)

        def op_ts(eng, out, in0, scalar1, op0, reads, writes):
            P.emit(eng, lambda e: e.tensor_scalar(out=out, in0=in0, scalar1=scalar1, scalar2=None, op0=op0), reads, writes)

        def op_copy(eng, out, in_, reads, writes):
            P.emit(eng, lambda e: e.tensor_copy(out=out, in_=in_), reads, writes)

        def op_recip(out, in_, reads, writes):
            P.emit("dve", lambda e: e.reciprocal(out=out, in_=in_), reads, writes)

        def op_memset(eng, ap, val, writes):
            P.emit(eng, lambda e: e.memset(ap, val), (), writes)

        def op_mm(mms, reads, writes):
            mms = list(mms)

            def fn(e):
                last = None
                for (o, l, r, st, sp_) in mms:
                    last = e.matmul(o, lhsT=l, rhs=r, start=st, stop=sp_)
                return last
            fn.n_mm = len(mms)
            P.emit("pe", fn, reads, writes)

        def op_tr(trs, reads, writes):
            trs = list(trs)

            def fn(e):
                last = None
                for (o, i, idn) in trs:
                    last = e.transpose(o, i, idn)
                return last
            fn.n_mm = len(trs)
            P.emit("pe", fn, reads, writes)

        def op_dma(pairs, res, reads=(), writes=(), eng="sp"):
            pairs = list(pairs)
            P.emit(eng, lambda e: [e.dma_start(out=d, in_=s_) for (d, s_) in pairs], reads, writes, dma_res=res,
                   n_dma=len(pairs))

        out_res = {}

        def dma_out(dst, src, src_res):
            op_dma([(dst, src)], src_res, reads=[src_res])
            out_res[src_res.name] = src_res

        def dbg_dump(slot, subtiles):
            if not debug:
                return
            if slot >= 4:
                return
            rr = [rX[k][s_] for k in range(KC) for s_ in range(len(subtiles))]
            op_dma([(dbg[slot], Xt[:, :])], rX[0][0], reads=rr)
            out_res[rX[0][0].name] = rX[0][0]

        rDBG = Res("DBG")

        def dbg_raw(slot, lo, hi):
            if not debug:
                return
            P.barrier()
            op_copy("dve", TMP16[:, :], TMP16[:, :], [rTMP16], [rTMP16])
            op_dma([(dbg[slot][:, 0:hi - lo], arena[:, lo:hi])], rDBG, reads=[rTMP16])
            out_res[rDBG.name] = rDBG
            op_copy("dve", TMP16[:, :], TMP16[:, :], [rTMP16], [rTMP16, rDBG])
            P.barrier()

        op_memset("pool", onesf[:, :], 1.0, [rID])
        op_memset("pool", onesb[:, :], 1.0, [rID])
        P.emit("pool", lambda e: e.affine_select(out=ident[:, :], in_=onesf[:, :], pattern=[[-1, 128]],
                                                 compare_op=ALU.is_equal, fill=0.0, base=0, channel_multiplier=1),
               (), [rID])
        P.emit("pool", lambda e: e.affine_select(out=maskt[:, :], in_=onesf[:, :], pattern=[[1, 128]],
                                                 compare_op=ALU.is_ge, fill=0.0, base=0, channel_multiplier=-1),
               (), [rID])
        op_memset("pool", HALOP[:, :], 0.0, rHALOP)
        op_memset("pool", HALOC[:, :], 0.0, rHALOC)
        P.emit("pool", lambda e: e.iota(INVCi[:, :], pattern=[[1, 16]], base=1, channel_multiplier=0), (), [rINV])
        for g, w in enumerate(WINS):
            op_ts("dve", INVC[:, g * 16:(g + 1) * 16], INVCi[:, :], float(w), ALU.min, [rINV], [rINV])
        op_recip(INVC[:, :], INVC[:, :], [rINV], [rINV])

        VEC = XTt[1]
        rVEC = rXT[1]
        op_memset("pool", VEC[0:16, :], 0.0, [rVEC])
        vec_rows = [(g_mix_pre, 0), (g_mix_pre, 1), (g_mix_post, 0), (g_mix_post, 1),
                    (g_ffn_pre, 0), (g_ffn_pre, 1), (g_ffn_post, 0), (g_ffn_post, 1)]
        pairs = []
        for r, (t, l) in enumerate(vec_rows):
            pairs.append((VEC[r:r + 1, :], t[l:l + 1, :]))
        pairs.append((VEC[8:9, 0:512], pool_scale[0:1, :]))
        pairs.append((VEC[8:9, 512:1024], conv_w[0:1, :]))
        pairs.append((VEC[9:10, 0:512], conv_w[1:2, :]))
        pairs.append((VEC[9:10, 512:1024], conv_w[2:3, :]))
        op_dma(pairs, rVEC, writes=[rVEC], eng="act")
        for half in range(2):
            b = next_bank()
            op_tr([(PSt[b][:, j * 16:(j + 1) * 16], VEC[0:16, (half * 4 + j) * 128:(half * 4 + j + 1) * 128], ident[0:16, 0:16])
                   for j in range(4)], [rVEC, rID], [rPS[b]])
            op_act(GT[:, half * 64:(half + 1) * 64], PSt[b][:, 0:64], AF.Copy, [rPS[b]], [rC])
        G_MIX_PRE, G_MIX_POST, G_FFN_PRE, G_FFN_POST = 0, 2, 4, 6
        for kc in range(KC):
            op_recip(GTIv[:, kc, 0:8], GTv[:, kc, 0:8], [rC], [rC])

        def gcol(row, kc):
            return GTv[:, kc, row:row + 1]

        def pool_scale_col(g):
            return GTv[:, g, 8:9]

        def conv_col(k, i):
            if k == 0:
                return GTv[:, 4 + i, 8:9]
            if k == 1:
                return GTv[:, i, 9:10]
            return GTv[:, 4 + i, 9:10]

        BIASPv = BIASP[:, :].rearrange("p (h t) -> p h t", t=128)
        XT0v = XTt[0][:, :].rearrange("p (h s) -> p h s", s=128)
        WMTPv = WMTP[:, :].rearrange("p (h t) -> p h t", t=128)
        WMTSv = WMTS[:, :].rearrange("p (h t) -> p h t", t=128)
        SPTv = SPT[:, :].rearrange("p (g n) -> p g n", n=240)
        SCTv = SCT[:, :].rearrange("p (g n) -> p g n", n=32)

        WPG_st = WBFt[1][:, :].bitcast(F32)
        BD_st = WBFt[2][:, :].bitcast(F32)
        WSP_st = RAt[:, 0:1024]
        rWSP = [rRA[0], rRA[1], rRA[2]]
        rWPG = [rWBF[1], rWBFb[1]]
        rBD = [rWBF[2], rWBFb[2]]

        def setup_dmas(deng):
            op_dma([(WPG_st[:, 0:512].rearrange("p (g d) -> p g d", d=128), w_pool_grp.rearrange("g c d -> c g d"))],
                   rWBF[1], writes=rWPG, eng=deng)
            op_dma([(WSP_st.rearrange("p (h s) -> p h s", s=128), w_spatial.rearrange("h t s -> t h s"))],
                   rRA[0], writes=rWSP, eng=deng)
            op_memset("pool", BD_st, 0.0, rBD)
            BDv = BD_st.rearrange("p (h s) -> p h s", s=128)
            op_dma([(BDv[bb * 8:(bb + 1) * 8, :, bb * 8:(bb + 1) * 8], w_spatial[:, 0:8, 0:8].rearrange("h t s -> t h s"))
                    for bb in range(16)], rWBF[2], writes=rBD, eng=deng)
            op_dma([(BIASP[:, :], b_spatial.rearrange("a b -> (a b)").partition_broadcast(128)),
                    (GVBC[:, :], g_v.rearrange("a b -> (a b)").partition_broadcast(128))], rBIAS, writes=[rBIAS], eng=deng)

        def setup_late():
            op_copy("pool", WGt[:, :], WPG_st[:, 0:512], rWPG, [rWG])

        def setup_late_b():
            def spatial_setup(src_xt, dstT, rsrc):
                for half in range(2):
                    b = next_bank()
                    op_tr([(PSt[b][:, j * 128:(j + 1) * 128], src_xt[:, (half * 4 + j) * 128:(half * 4 + j + 1) * 128], ident[:, :])
                           for j in range(4)], rsrc + [rID], [rPS[b]])
                    for j in range(4):
                        hh = half * 4 + j
                        op_tt("dve", dstT[:, hh * 128:(hh + 1) * 128], PSt[b][:, j * 128:(j + 1) * 128], maskt[:, :], ALU.mult,
                              [rPS[b], rID], [rWMT])
            spatial_setup(WSP_st, WMTP, rWSP)
            spatial_setup(BD_st, WMTS, rBD)

        def state_dmas():
            op_dma([(XTt[0][0:120, 0:512], sp_in[0:120, :]), (XTt[0][0:32, 512:1024], sc_in[:, :])], rXT[0], writes=[rXT[0]])
            op_dma([(XTt[1][0:120, 0:512], sp_in[120:240, :])], rXT[1], writes=[rXT[1]])

        def state_compute():
            for half in range(2):
                b = next_bank()
                op_tr([(PSt[b][:, g * 128:g * 128 + 120], XTt[half][0:120, g * 128:(g + 1) * 128], ident[0:120, 0:120])
                       for g in range(4)], [rXT[half], rID], [rPS[b]])
                op_act(SPTv[:, :, half * 120:(half + 1) * 120], PSt[b][:, :].rearrange("p (g n) -> p g n", n=128)[:, :, 0:120],
                       AF.Copy, [rPS[b]], rSPT)
            b = next_bank()
            op_tr([(PSt[b][:, g * 128:g * 128 + 32], XTt[0][0:32, 512 + g * 128:512 + (g + 1) * 128], ident[0:32, 0:32]) for g in range(4)],
                  [rXT[0], rID], [rPS[b]])
            op_act(SCTv[:, :, :], PSt[b][:, :].rearrange("p (g n) -> p g n", n=128)[:, :, 0:32], AF.Copy, [rPS[b]], rSCT)

        units = []

        def colblock(Wap, cols, grow=None):
            def mk(wst):
                v = wst[:, :].rearrange("p (k n) -> p k n", n=256)
                res = []
                off = 0
                for (c0, n) in cols:
                    res.append((v[:, :, off:off + n], Wap.rearrange("(k p) n -> p k n", p=128)[:, :, c0:c0 + n]))
                    off += n
                return res
            mk.grow = grow
            return mk

        def downblock(Wap, row0, c):
            def mk(wst):
                v = wst[:, :].rearrange("p (k n) -> p k n", n=128)
                return [(v, Wap[row0:row0 + 2048, c * 128:(c + 1) * 128].rearrange("(k p) n -> p k n", p=128))]
            mk.grow = None
            return mk

        def layer_units(l):
            us = []
            if l == 0:
                us.append(colblock(w_in_ab, [(0, 256)], 0))
                for i in (0, 1):
                    us.append(colblock(w_in_ab, [(512 + 128 * i, 128), (1536 + 128 * i, 128)], 0))
                us.append(colblock(w_in_ab, [(256, 256)], 0))
                for i in (2, 3):
                    us.append(colblock(w_in_ab, [(512 + 128 * i, 128), (1536 + 128 * i, 128)], 0))
                us.append(colblock(w_in_ab, [(1024, 256)], 0))
                us.append(colblock(w_in_ab, [(1280, 256)], 0))
                for cb in range(4):
                    us.append(colblock(w_out_ab, [(cb * 256, 256)]))
            else:
                for vb in range(4):
                    us.append(colblock(w_uv, [(1024 + vb * 256, 256)], 1))
                for ub in range(4):
                    us.append(colblock(w_uv, [(ub * 256, 256)], 1))
                for cb in range(4):
                    us.append(colblock(w_out_c, [(cb * 256, 256)]))
            for half in range(2):
                for ub in range(8):
                    us.append(colblock(w_up[l], [(half * 2048 + ub * 256, 256)], 4 + l))
                for c in range(8):
                    us.append(downblock(w_down[l], half * 2048, c))
            return us

        for grp in range(2):
            for l in range(2):
                units.extend(layer_units(l))
        NU = len(units)
        ws = {"dma": 0, "conv": 0, "next": 0}

        def w_next():
            i = ws["next"]
            ws["next"] += 1
            w_advance(min(i + 4, NU - 1), min(i + 2, NU - 1))
            return [rWBF[i % 3], rWBFb[i % 3]], WBFt[i % 3]

        def w_advance(md, mc, dma_eng="sp"):
            while True:
                if ws["dma"] <= md and ws["dma"] - 2 < ws["conv"]:
                    j = ws["dma"]
                    op_dma(units[j](WSTt[j % 2]), rWST[j % 2], writes=[rWST[j % 2]], eng=dma_eng)
                    ws["dma"] += 1
                elif ws["conv"] <= mc and ws["conv"] < ws["dma"]:
                    j = ws["conv"]
                    grow = None
                    if ws.get("all_dve"):
                        op_copy("dve", WBFt[j % 3][:, 0:1024], WSTt[j % 2][:, 0:1024], [rWST[j % 2]], [rWBF[j % 3]])
                        op_copy("dve", WBFt[j % 3][:, 1024:2048], WSTt[j % 2][:, 1024:2048], [rWST[j % 2]], [rWBFb[j % 3]])
                    elif grow is None:
                        op_act(WBFt[j % 3][:, 0:1024], WSTt[j % 2][:, 0:1024], AF.Copy, [rWST[j % 2]], [rWBF[j % 3]])
                        op_copy("dve", WBFt[j % 3][:, 1024:2048], WSTt[j % 2][:, 1024:2048], [rWST[j % 2]], [rWBFb[j % 3]])
                    else:
                        for kc in range(4):
                            op_act(WBFt[j % 3][:, kc * 256:(kc + 1) * 256], WSTt[j % 2][:, kc * 256:(kc + 1) * 256], AF.Copy,
                                   [rWST[j % 2], rC], [rWBF[j % 3]], scale=gcol(grow, kc))
                        for kc in range(4, 8):
                            op_ts("dve", WBFt[j % 3][:, kc * 256:(kc + 1) * 256], WSTt[j % 2][:, kc * 256:(kc + 1) * 256],
                                  gcol(grow, kc), ALU.mult, [rWST[j % 2], rC], [rWBFb[j % 3]])
                    ws["conv"] += 1
                else:
                    break

        def rstd_chain(s, c0, n, dst_t, dst_r, sbuf_copy, ob=None):
            bk = STAT[s]
            if ob is None:
                ob = bk
            op_act(dst_t[:, c0:c0 + n], PSt[bk][:, 0:n], AF.Ln, [rPS[bk]], [dst_r], scale=1.0 / D, bias=EPS)
            op_act(PSt[ob][:, 0:n], dst_t[:, c0:c0 + n], AF.Exp, [dst_r], [rPS[ob]], scale=-0.5)
            if sbuf_copy:
                op_act(dst_t[:, c0:c0 + n], dst_t[:, c0:c0 + n], AF.Exp, [dst_r], [dst_r], scale=-0.5)

        def stat_mm(s, n, sqi, first, last):
            bk = STAT[s]
            op_mm([(PSt[bk][:, 0:n], onesb[:, :], SQt[sqi][:, 0:n], first, last)], [rSQ[sqi], rID], [rPS[bk]])

        def m_stat(c, s, c0, n, src, src_res, scale=None):
            sqi = next_sq()
            if scale is None:
                op_act(SQt[sqi][:, 0:n], src, AF.Square, src_res, [rSQ[sqi]])
            else:
                op_act(SQt[sqi][:, 0:n], src, AF.Square, src_res + [rC], [rSQ[sqi]], scale=scale)
            deferred.append(lambda: stat_mm(s, n, sqi, c == 0, c == KC - 1))

        def split_eng(kc):
            return "pool" if kc in (2, 5, 7) else "dve"

        def pre_norm_stats(s, c0, n):
            for kc in range(KC):
                sqi = next_sq()
                op_act(SQt[sqi][:, 0:n], Xv[:, kc, c0:c0 + n], AF.Square, [rX[kc][s]], [rSQ[sqi]])
                stat_mm(s, n, sqi, kc == 0, kc == KC - 1)

        def pre_norm_apply(s, c0, n, g_pre_row):
            bk = STAT[s]
            for kc in range(KC):
                op_stt(Hv[:, kc, c0:c0 + n], Xv[:, kc, c0:c0 + n], gcol(g_pre_row, kc), PSt[bk][:, 0:n], ALU.mult, ALU.mult,
                       [rX[kc][s], rPS[bk], rC], [rH[kc][s]])

        def pre_norm(s, c0, n, g_pre_row):
            pre_norm_stats(s, c0, n)
            rstd_chain(s, c0, n, RBt, rRB[s], False)
            pre_norm_apply(s, c0, n, g_pre_row)

        tstate = {"free": [0, 1], "i": 0}

        def next_tbank():
            f = tstate["free"]
            b = f[tstate["i"] % len(f)]
            tstate["i"] += 1
            return b

        def post_part(s, c0, n, do_pre, bk):
            for kc in range(KC):
                if split_eng(kc) == "pool":
                    op_tt("pool", Mv[:, kc, c0:c0 + n], Mv[:, kc, c0:c0 + n], RAt[:, c0:c0 + n], ALU.mult,
                          [rM[kc][s], rRA[s]], [rM[kc][s]])
                    op_tt("pool", Xv[:, kc, c0:c0 + n], Xv[:, kc, c0:c0 + n], Mv[:, kc, c0:c0 + n], ALU.add,
                          [rM[kc][s], rX[kc][s]], [rX[kc][s]])
                else:
                    tb = next_tbank()
                    op_tt("dve", PSt[tb][:, 0:n], Mv[:, kc, c0:c0 + n], PSt[bk][:, 0:n], ALU.mult,
                          [rM[kc][s], rPS[bk]], [rPS[tb]])
                    op_tt("dve", Xv[:, kc, c0:c0 + n], Xv[:, kc, c0:c0 + n], PSt[tb][:, 0:n], ALU.add,
                          [rPS[tb], rX[kc][s]], [rX[kc][s]])
                if do_pre:
                    sqi = next_sq()
                    op_act(SQt[sqi][:, 0:n], Xv[:, kc, c0:c0 + n], AF.Square, [rX[kc][s]], [rSQ[sqi]])
                    stat_mm(s, n, sqi, kc == 0, kc == KC - 1)

        def post_norm_residual(subtiles, g_pre_row):
            do_pre = g_pre_row is not None
            ns = len(subtiles)
            rab = []
            for s, (c0, n) in enumerate(subtiles):
                rab.append(next_bank())
                rstd_chain(s, c0, n, RAt, rRA[s], True, ob=rab[s])
            tstate["free"] = [b_ for b_ in range(5) if b_ not in rab]
            post_part(0, subtiles[0][0], subtiles[0][1], do_pre, rab[0])
            if do_pre:
                rstd_chain(0, subtiles[0][0], subtiles[0][1], RBt, rRB[0], False)
            for s in range(ns):
                if s + 1 < ns:
                    post_part(s + 1, subtiles[s + 1][0], subtiles[s + 1][1], do_pre, rab[s + 1])
                    if do_pre:
                        rstd_chain(s + 1, subtiles[s + 1][0], subtiles[s + 1][1], RBt, rRB[s + 1], False)
                if do_pre:
                    pre_norm_apply(s, subtiles[s][0], subtiles[s][1], g_pre_row)

        def proj_group(wv, cc, rw, rhs_v, rhs_r, s, c0, n, nk=KC):
            b = next_bank()
            op_mm([(PSt[b][:, 0:n], wv[:, k, cc * 128:(cc + 1) * 128], rhs_v[:, k, c0:c0 + n], k == 0, k == nk - 1)
                   for k in range(nk)], rw + [rhs_r[k][s] for k in range(nk)], [rPS[b]])
            return b

        def proj_to_M(wv, rw, rhs_v, rhs_r, subtiles, cbase, g_post_row, accumulate=False, nk=KC, ncc=2, stats=True):
            for s, (c0, n) in enumerate(subtiles):
                for cc in range(ncc):
                    c = cbase + cc
                    b = proj_group(wv, cc, rw, rhs_v, rhs_r, s, c0, n, nk)
                    flush_deferred()
                    if not accumulate:
                        op_act(Mv[:, c, c0:c0 + n], PSt[b][:, 0:n], AF.Copy, [rPS[b], rC], [rM[c][s]], scale=gcol(g_post_row, c))
                        if stats:
                            m_stat(c, s, c0, n, PSt[b][:, 0:n], [rPS[b]])
                    else:
                        op_stt(Mv[:, c, c0:c0 + n], PSt[b][:, 0:n], gcol(g_post_row, c), Mv[:, c, c0:c0 + n], ALU.mult, ALU.add,
                               [rPS[b], rM[c][s], rC], [rM[c][s]])
                        if stats:
                            m_stat(c, s, c0, n, Mv[:, c, c0:c0 + n], [rM[c][s]], scale=GTIv[:, c, g_post_row:g_post_row + 1])

        def run_group(gi):
            has_s = (gi == 1)
            state["banks"] = (0, 1, 2, 3, 4) if has_s else (0, 1, 2, 3, 4, 7)
            T = 1152 if has_s else 1024
            subtiles = [(0, 512), (512, 512)] + ([(1024, 128)] if has_s else [])
            nchunks = T // 128
            prow0 = gi * 1024
            P.barrier()

            if gi == 0:
                w_advance(0, -1, dma_eng="act")
            P.phase = f"g{gi}.load"
            for n in range(nchunks):
                xi = next_xt()
                src = xp[prow0 + n * 128: prow0 + (n + 1) * 128, :] if n < 8 else xs[:, :]
                op_dma([(XTt[xi][:, :], src)], rXT[xi], writes=[rXT[xi]])
                s = n // 4
                for half in range(2):
                    b = next_bank()
                    op_tr([(PSt[b][:, j * 128:(j + 1) * 128], XTt[xi][:, (half * 4 + j) * 128:(half * 4 + j + 1) * 128], ident[:, :])
                           for j in range(4)], [rXT[xi], rID], [rPS[b]])
                    op_act(Xv[:, half * 4:(half + 1) * 4, n * 128:(n + 1) * 128], PSt[b][:, :].rearrange("p (j t) -> p j t", t=128),
                           AF.Copy, [rPS[b]], [rX[half * 4 + j][s] for j in range(4)])
                if n == min(4 * s + 3, nchunks - 1):
                    pre_norm(s, subtiles[s][0], subtiles[s][1], G_MIX_PRE + 0)

            if gi == 0:
                setup_dmas("sp")
                w_advance(1, -1, dma_eng="sp")
            P.phase = f"g{gi}.prenorm0"
            if gi == 0:
                P.phase = "setup_late"
                setup_late()
                w_advance(3, 1)
                setup_late_b()

            for l in range(2):
                P.phase = f"g{gi}.L{l}.mixer"
                if gi == 0 and l == 1:
                    state_dmas()
                if l == 0:
                    mixer0(gi, has_s, subtiles)
                else:
                    mixer1(gi, has_s, subtiles, nchunks)
                flush_deferred()
                P.phase = f"g{gi}.L{l}.bnd_mix"
                post_norm_residual(subtiles, G_FFN_PRE + l)
                if gi == 1 and l == 0:
                    emit_state_outputs()
                dbg_dump(gi * 4 + l * 2, subtiles)
                P.barrier()
                P.phase = f"g{gi}.L{l}.ffn"
                if gi == 0 and l == 1:
                    state_compute()
                ffn(l, subtiles)
                flush_deferred()
                P.phase = f"g{gi}.L{l}.bnd_ffn"
                post_norm_residual(subtiles, (G_MIX_PRE + l + 1) if l == 0 else None)
                dbg_dump(gi * 4 + l * 2 + 1, subtiles)
                if l == 0:
                    P.barrier()

            P.phase = f"g{gi}.out"
            ob = {"i": 0}
            for n in range(nchunks):
                xi = next_xt()
                s = n // 4
                for half in range(2):
                    b = STAT[ob["i"] % 3]
                    ob["i"] += 1
                    op_tr([(PSt[b][:, j * 128:(j + 1) * 128], Xv[:, half * 4 + j, n * 128:(n + 1) * 128], ident[:, :])
                           for j in range(4)], [rX[half * 4 + j][s] for j in range(4)] + [rID], [rPS[b]])
                    op_act(XTt[xi][:, half * 512:(half + 1) * 512], PSt[b][:, :], AF.Copy, [rPS[b]], [rXT[xi]])
                dst = yp[prow0 + n * 128: prow0 + (n + 1) * 128, :] if n < 8 else ys[:, :]
                dma_out(dst, XTt[xi][:, :], rXT[xi])

        def useg(buf):
            return buf[:, 0:1039], buf[:, 1040:1408].rearrange("p (b l) -> p b l", l=23)

        def zseg(buf):
            return buf[:, 0:1026], buf[:, 1028:1188].rearrange("p (b l) -> p b l", l=10)

        def mixer0(gi, has_s, subtiles):
            Tp = 1024

            def pool_math(g):
                r = g % 2
                w = WINS[g]
                Up, Us = useg(Ub[r])
                S1p, S1s = useg(S1b)
                S2p, S2s = useg(S2b)
                S3p, S3s = useg(S3b)
                src_p, src_s, src_r = Up, Us, [rU[r]]
                if g % 2 == 0:
                    bufs = [(S1p, S1s, [rS1]), (S2p, S2s, [rS2])]
                else:
                    bufs = [(S2p, S2s, [rS2]), (S3p, S3s, [rCZ[2], rCZ[3]])]
                sh = 1
                for lev in range(g + 1):
                    dp, dsv, dr = bufs[lev % 2]
                    jmin = 2 * sh - 1
                    op_tt("pool", dp[:, jmin:1039], src_p[:, jmin:1039], src_p[:, jmin - sh:1039 - sh], ALU.add, src_r, dr)
                    if has_s:
                        op_tt("pool", dsv[:, :, jmin:23], src_s[:, :, jmin:23], src_s[:, :, jmin - sh:23 - sh], ALU.add, src_r, dr)
                    src_p, src_s, src_r = dp, dsv, dr
                    sh *= 2
                Dp = Db[r][:, 0:1024]
                Dsv = Db[r][:, 1024:1152].rearrange("p (b t) -> p b t", t=8)
                def dve_part():
                    if gi == 0:
                        op_tt("dve", TMP16[:, :], src_p[:, 15:31], INVC[:, g * 16:(g + 1) * 16], ALU.mult, src_r + [rINV], [rTMP16])
                    op_stt(Dp, src_p[:, 15:1039], 1.0 / w, Up[:, 15:1039], ALU.mult, ALU.subtract, src_r + [rU[r]], [rD[r]])
                    if gi == 0:
                        op_tt("dve", Db[r][:, 0:16], TMP16[:, :], Up[:, 15:31], ALU.subtract, [rTMP16, rU[r]], [rD[r]])
                    if has_s:
                        op_stt(Dsv, src_s[:, :, 15:23], 1.0 / w, Us[:, :, 15:23], ALU.mult, ALU.subtract, src_r + [rU[r]], [rD[r]])
                    op_copy("dve", HALOP[:, g * 15:(g + 1) * 15], Up[:, Tp:Tp + 15], [rU[r]], [rHALOP[g]])
                    if has_s:
                        op_copy("dve", SPT[:, g * 240:(g + 1) * 240].rearrange("p (b l) -> p b l", l=15), Us[:, :, 8:23],
                                [rU[r]], [rSPT[g]])

                def pe_part():
                    for s, (c0, n) in enumerate(subtiles):
                        b = next_bank()
                        op_mm([(PSt[b][:, 0:n], WGv[:, g, :], Db[r][:, c0:c0 + n], True, True)], [rD[r], rWG], [rPS[b]])
                        op_act(Yv[:, g, c0:c0 + n], PSt[b][:, 0:n], AF.Copy, [rPS[b], rC], [rY[g][s]], scale=pool_scale_col(g))
                deferred.append(lambda: (dve_part(), pe_part()))

            def conv_math(i):
                r = i % 2
                Zp, Zs = zseg(Zb[r])
                CZp = CZv[:, i, 0:1024]
                CZs = CZv[:, i, 1024:1152].rearrange("p (b t) -> p b t", t=8)

                def sl(v, a, bnd):
                    return v[:, a:bnd] if len(v.shape) == 2 else v[:, :, a:bnd]
                T1p, T1s = S1b[:, 0:1024], S1b[:, 1040:1168].rearrange("p (b t) -> p b t", t=8)
                T2p, T2s = S2b[:, 0:1024], S2b[:, 1040:1168].rearrange("p (b t) -> p b t", t=8)
                segs = [(Zp, CZp, T1p, T2p, 1024)] + ([(Zs, CZs, T1s, T2s, 8)] if has_s else [])
                for (zv, cv, t1, t2, L) in segs:
                    op_ts("dve", cv, sl(zv, 0, L), conv_col(0, i), ALU.mult, [rZ[r], rC], [rCZ[i]])
                    for k in (1, 2):
                        op_stt(cv, sl(zv, k, L + k), conv_col(k, i), cv, ALU.mult, ALU.add, [rZ[r], rCZ[i], rC], [rCZ[i]])
                op_copy("dve", HALOC[:, i * 2:(i + 1) * 2], Zp[:, 1024:1026], [rZ[r]], [rHALOC[i]])
                if has_s:
                    op_copy("dve", SCT[:, i * 32:(i + 1) * 32].rearrange("p (b l) -> p b l", l=2), Zs[:, :, 8:10], [rZ[r]], [rSCT[i]])

            def u_unit(g0):
                rw, wt = w_next()
                wv = wt[:, :].rearrange("p (k n) -> p k n", n=256)
                for cc in range(2):
                    g = g0 + cc
                    r = g % 2
                    Up, Us = useg(Ub[r])
                    op_act(Up[:, 0:15], HALOP[:, g * 15:(g + 1) * 15], AF.Copy, [rHALOP[g]], [rU[r]])
                    if has_s:
                        op_act(Us[:, :, 0:15], SPT[:, g * 240:(g + 1) * 240].rearrange("p (b l) -> p b l", l=15), AF.Copy,
                                [rSPT[g]], [rU[r]])
                    for s, (c0, n) in enumerate(subtiles):
                        b = proj_group(wv, cc, rw, Hv, rH, s, c0, n)
                        if s < 2:
                            op_act(Up[:, 15 + c0:15 + c0 + n], PSt[b][:, 0:n], AF.Copy, [rPS[b]], [rU[r]])
                        else:
                            op_act(Us[:, :, 15:23], PSt[b][:, 0:128].rearrange("p (b t) -> p b t", t=8), AF.Copy, [rPS[b]], [rU[r]])
                    pool_math(g)

            def a_unit(i):
                rw, wt = w_next()
                wv = wt[:, :].rearrange("p (k n) -> p k n", n=256)
                r = i % 2
                Zp, Zs = zseg(Zb[r])
                op_act(Zp[:, 0:2], HALOC[:, i * 2:(i + 1) * 2], AF.Copy, [rHALOC[i]], [rZ[r]])
                if has_s:
                    op_act(Zs[:, :, 0:2], SCT[:, i * 32:(i + 1) * 32].rearrange("p (b l) -> p b l", l=2), AF.Copy, [rSCT[i]], [rZ[r]])
                for s, (c0, n) in enumerate(subtiles):
                    if s < 2:
                        zdst = Zp[:, 2 + c0:2 + c0 + n]
                        psv = lambda b, n=n: PSt[b][:, 0:n]
                    else:
                        zdst = Zs[:, :, 2:10]
                        psv = lambda b: PSt[b][:, 0:128].rearrange("p (b t) -> p b t", t=8)
                    b1 = proj_group(wv, 0, rw, Hv, rH, s, c0, n)
                    op_act(zdst, psv(b1), AF.Copy, [rPS[b1]], [rZ[r]])
                    b2 = proj_group(wv, 1, rw, Hv, rH, s, c0, n)
                    op_tt("dve", zdst, psv(b2), zdst, ALU.mult, [rPS[b2], rZ[r]], [rZ[r]])
                flush_deferred()
                conv_math(i)

            def gb_unit(i0):
                rw, wt = w_next()
                wv = wt[:, :].rearrange("p (k n) -> p k n", n=256)
                for cc in range(2):
                    i = i0 + cc
                    for s, (c0, n) in enumerate(subtiles):
                        b = proj_group(wv, cc, rw, Hv, rH, s, c0, n)
                        op_tt("dve", Yv[:, 4 + i, c0:c0 + n], PSt[b][:, 0:n], CZv[:, i, c0:c0 + n], ALU.mult,
                              [rPS[b], rCZ[i]], [rY[4 + i][s]])

            u_unit(0)
            a_unit(0)
            flush_deferred()
            a_unit(1)
            u_unit(2)
            a_unit(2)
            flush_deferred()
            a_unit(3)
            gb_unit(0)
            gb_unit(2)
            flush_deferred()
            P.barrier()
            P.phase = f"g{gi}.L0.mixer.wout"
            ws["all_dve"] = True
            for cb in range(4):
                rw, wt = w_next()
                wv = wt[:, :].rearrange("p (k n) -> p k n", n=256)
                proj_to_M(wv, rw, Yv, rY, subtiles, cb * 2, G_MIX_POST + 0)
            ws["all_dve"] = False

        def emit_state_outputs():
            stg = [(XTt[0], rXT[0]), (XTt[1], rXT[1]), (XTt[2], rXT[2]), (XTt[0], rXT[0])]
            cnt = {"i": 0}

            def one(srcs, nrows, dst, reads):
                b = next_bank()
                st_t, st_r = stg[cnt["i"] % 4]
                cnt["i"] += 1
                op_tr([(PSt[b][0:nrows, g * 128:(g + 1) * 128], srcs[g], ident[:, :]) for g in range(4)], reads + [rID], [rPS[b]])
                op_act(st_t[0:nrows, 0:512], PSt[b][0:nrows, :], AF.Copy, [rPS[b]], [st_r])
                dma_out(dst, st_t[0:nrows, 0:512], st_r)
            one([HALOP[:, g * 15:(g + 1) * 15] for g in range(4)], 15, o_pool_p[:, :], rHALOP)
            one([HALOC[:, g * 2:(g + 1) * 2] for g in range(4)], 2, o_conv_p[:, :], rHALOC)
            one([SCT[:, g * 32:(g + 1) * 32] for g in range(4)], 32, o_conv_s[:, :], rSCT)
            for half in range(2):
                one([SPT[:, g * 240 + half * 120: g * 240 + (half + 1) * 120] for g in range(4)], 120,
                    o_pool_s[half * 120:(half + 1) * 120, :], rSPT)

        def mixer1(gi, has_s, subtiles, nchunks):
            for vb in range(4):
                rw, wt = w_next()
                wv = wt[:, :].rearrange("p (k n) -> p k n", n=256)
                for n in range(nchunks):
                    s = n // 4
                    b = next_bank()
                    op_mm([(PSt[b][:, 0:256], Hv[:, k, n * 128:(n + 1) * 128], wv[:, k, :], k == 0, k == KC - 1) for k in range(KC)],
                          rw + [rH[k][s] for k in range(KC)], [rPS[b]])
                    op_act(Vv[:, n, vb * 256:(vb + 1) * 256], PSt[b][:, 0:256], AF.Copy, [rPS[b]], [rV[n]])
                    if vb == 3:
                        op_act(VNv[:, n, :], Vv[:, n, :], AF.Square, [rV[n]], [rVN[n], rSSn[n]], accum_out=SS[:, n:n + 1])
                        op_act(RV[:, n:n + 1], SS[:, n:n + 1], AF.Ln, [rSSn[n]], [rSSn[n]], scale=1.0 / D, bias=EPS)
                        op_act(RV[:, n:n + 1], RV[:, n:n + 1], AF.Exp, [rSSn[n]], [rSSn[n]], scale=-0.5)
                        op_stt(VNv[:, n, :], Vv[:, n, :], RV[:, n:n + 1], GVBC[:, :], ALU.mult, ALU.mult,
                               [rV[n], rSSn[n], rBIAS], [rVN[n]])
                        if n == 8:
                            xi = next_xt()
                            op_stt(XTt[xi][:, :], Vv[:, n, :], RV[:, n:n + 1], GVBC[:, :], ALU.mult, ALU.mult,
                                   [rV[n], rSSn[n], rBIAS], [rXT[xi]])
                            dma_out(o_v_s[:, :], XTt[xi][:, :], rXT[xi])
            P.phase = f"g{gi}.L1.mixer.vnorm"
            P.barrier()
            if gi == 0:
                dbg_raw(5, 13824, 18432)
            for s, (c0, n) in enumerate(subtiles):
                for hh in range(KC):
                    b = next_bank()
                    if s < 2:
                        op_mm([(PSt[b][:, j * 128:(j + 1) * 128], VNv[:, s * 4 + j, hh * 128:(hh + 1) * 128], WMTPv[:, hh, :], True, True)
                               for j in range(4)], [rVN[s * 4 + j] for j in range(4)] + [rWMT], [rPS[b]])
                        bias_v = BIASPv[:, hh, :].unsqueeze(1).broadcast_to([128, 4, 128])
                        op_tt("dve", Mv[:, hh, c0:c0 + 512].rearrange("p (j t) -> p j t", t=128),
                              PSt[b][:, :].rearrange("p (j t) -> p j t", t=128), bias_v, ALU.add, [rPS[b], rBIAS], [rSVB[hh][s]])
                    else:
                        op_mm([(PSt[b][:, 0:128], VNv[:, 8, hh * 128:(hh + 1) * 128], WMTSv[:, hh, :], True, True)],
                              [rVN[8], rWMT], [rPS[b]])
                        op_tt("dve", Mv[:, hh, c0:c0 + 128].rearrange("p (b t) -> p b t", t=8),
                              PSt[b][:, 0:128].rearrange("p (b t) -> p b t", t=8),
                              BIASPv[:, hh, 0:8].unsqueeze(1).broadcast_to([128, 16, 8]), ALU.add, [rPS[b], rBIAS], [rSVB[hh][s]])
            if gi == 0:
                dbg_raw(6, 4608, 13824)
            P.phase = f"g{gi}.L1.mixer.u"
            for ub in range(4):
                rw, wt = w_next()
                wv = wt[:, :].rearrange("p (k n) -> p k n", n=256)
                for s, (c0, n) in enumerate(subtiles):
                    for cc in range(2):
                        c = ub * 2 + cc
                        b = proj_group(wv, cc, rw, Hv, rH, s, c0, n)
                        op_tt("dve", Gv[:, c, c0:c0 + n], PSt[b][:, 0:n], Mv[:, c, c0:c0 + n], ALU.mult,
                              [rPS[b], rSVB[c][s]], [rG[c][s]])
            P.barrier()
            if gi == 0:
                dbg_raw(7, 18432, 23040)
            ws["all_dve"] = True
            for cb in range(4):
                rw, wt = w_next()
                wv = wt[:, :].rearrange("p (k n) -> p k n", n=256)
                proj_to_M(wv, rw, Gv, rG, subtiles, cb * 2, G_MIX_POST + 1)
            ws["all_dve"] = False

        def ffn(l, subtiles):
            for half in range(2):
                for ub in range(8):
                    rw, wt = w_next()
                    wv = wt[:, :].rearrange("p (k n) -> p k n", n=256)
                    for s, (c0, n) in enumerate(subtiles):
                        for cc in range(2):
                            j = ub * 2 + cc
                            b = proj_group(wv, cc, rw, Hv, rH, s, c0, n)
                            ri = next_rt()
                            op_act(RTt[ri][:, 0:n], PSt[b][:, 0:n], AF.Relu, [rPS[b]], [rRT[ri]])
                            op_tt("pool", HIDv[:, j, c0:c0 + n], RTt[ri][:, 0:n], RTt[ri][:, 0:n], ALU.mult, [rRT[ri]], [rHID[j][s]])
                for c in range(8):
                    rw, wt = w_next()
                    wv = wt[:, :].rearrange("p (k n) -> p k n", n=128)
                    proj_to_M(wv, rw, HIDv, rHID, subtiles, c, G_FFN_POST + l, accumulate=(half == 1), nk=16, ncc=1, stats=(half == 1))

        run_group(0)
        run_group(1)
        flush_deferred()
        P.emit("sp", None, (), list(out_res.values()))

        import os as _os
        if _os.environ.get("PE_LOG"):
            import json as _json
            _json.dump(P.pe_log, open(_os.environ["PE_LOG"], "w"))
        with nc.Block() as block:
            @block.sync
            def _(e):
                P.replay(e, "sp")

            @block.tensor
            def _(e):
                P.replay(e, "pe")

            @block.scalar
            def _(e):
                P.replay(e, "act")

            @block.vector
            def _(e):
                P.replay(e, "dve")

            @block.gpsimd
            def _(e):
                P.replay(e, "pool")
    return nc


_CACHE = {}


def kernel(x_prompt, x_sample, state_pool, state_conv,
           g_mix_pre, g_mix_post, g_ffn_pre, g_ffn_post,
           w_in_ab, w_pool_grp, pool_scale, conv_w, w_out_ab,
           w_uv, g_v, w_spatial, b_spatial, w_out_c, w_up, w_down):
    f = lambda a: np.ascontiguousarray(np.asarray(a, dtype=np.float32))
    if "nc" not in _CACHE:
        _CACHE["nc"] = build_program()
    nc = _CACHE["nc"]
    shared = {
        "g_mix_pre": f(g_mix_pre), "g_mix_post": f(g_mix_post), "g_ffn_pre": f(g_ffn_pre), "g_ffn_post": f(g_ffn_post),
        "w_in_ab": f(w_in_ab[0]), "w_pool_grp": f(w_pool_grp[0]), "pool_scale": f(pool_scale), "conv_w": f(conv_w[0]),
        "w_out_ab": f(w_out_ab[0]), "w_uv": f(w_uv[0]), "g_v": f(g_v), "w_spatial": f(w_spatial[0]),
        "b_spatial": f(b_spatial[0]), "w_out_c": f(w_out_c[0]), "w_up": f(w_up), "w_down": f(w_down),
    }
    xp = f(x_prompt)
    xsm = f(x_sample)
    spl = f(state_pool)
    scv = f(state_conv)
    in_maps = []
    for c in range(NCORES):
        m = dict(shared)
        m["xp"] = xp[c]
        m["xs"] = xsm[c * DEC_B:(c + 1) * DEC_B].reshape(128, D)
        m["sp"] = spl[0, c * DEC_B:(c + 1) * DEC_B].reshape(240, 512)
        m["sc"] = scv[0, c * DEC_B:(c + 1) * DEC_B].reshape(32, 512)
        in_maps.append(m)
    res = run_bass_kernel_spmd(nc, in_maps, core_ids=list(range(NCORES)))
    R = res.results
    y_prompt = np.stack([R[c]["yp"] for c in range(NCORES)], axis=0)
    y_sample = np.concatenate([R[c]["ys"].reshape(DEC_B, DEC_T, D) for c in range(NCORES)], axis=0)
    pool_p = np.stack([R[c]["o_pool_p"] for c in range(NCORES)], axis=0)[None]
    pool_s = np.concatenate([R[c]["o_pool_s"].reshape(DEC_B, 15, 512) for c in range(NCORES)], axis=0)[None]
    conv_p = np.stack([R[c]["o_conv_p"] for c in range(NCORES)], axis=0)[None]
    conv_s = np.concatenate([R[c]["o_conv_s"].reshape(DEC_B, 2, 512) for c in range(NCORES)], axis=0)[None]
    v_s = np.concatenate([R[c]["o_v_s"].reshape(DEC_B, DEC_T, D) for c in range(NCORES)], axis=0)[None]
    return (y_prompt.astype(np.float32), y_sample.astype(np.float32), pool_p.astype(np.float32), pool_s.astype(np.float32),
            conv_p.astype(np.float32), conv_s.astype(np.float32), v_s.astype(np.float32))
```

```python
from contextlib import ExitStack

import numpy as np
import concourse.bass as bass
import concourse.mybir as mybir
from concourse.bass_utils import run_bass_kernel_spmd

F32 = mybir.dt.float32
BF16 = mybir.dt.bfloat16
I32 = mybir.dt.int32
ALU = mybir.AluOpType
AF = mybir.ActivationFunctionType

NCORES = 8
D = 1024
KC = 8
SEQ = 2048
DEC_B = 16
DEC_T = 8
EPS = 1e-6
TM = 1152
WINS = (2, 4, 8, 16)
COMPUTE = ("pe", "act", "dve", "pool")
SELF_SYNC = True
FAST_RECIP = False


class Res:
    __slots__ = ("name", "last_w", "readers", "dma_count", "excl")

    def __init__(self, name, excl=False):
        self.name = name
        self.excl = excl
        self.last_w = None
        self.readers = []
        self.dma_count = 0


class Plan:
    def __init__(self, nc, es):
        self.nc = nc
        self.es = es
        self.ops = {e: [] for e in COMPUTE + ("sp",)}
        self.count = {e: 0 for e in COMPUTE}
        self.seen = {e: {} for e in COMPUTE + ("sp",)}
        self.pending = {e: [] for e in COMPUTE + ("sp",)}
        self.sems = {}
        self.phase = "setup"
        self.pe_log = []

    def sem(self, key):
        if key not in self.sems:
            nm = "s_" + (key if isinstance(key, str) else "d_" + key[1])
            self.sems[key] = self.es.enter_context(self.nc.semaphore(nm))
        return self.sems[key]

    def emit(self, eng, fn, reads=(), writes=(), dma_res=None, n_dma=1):
        deps = []
        for r in reads:
            if r.last_w is not None:
                deps.append(r.last_w)
            if r.excl:
                deps.extend(t for t in r.readers if t[0] != eng)
        for w in writes:
            if w.last_w is not None:
                deps.append(w.last_w)
            deps.extend(w.readers)
        deps.extend(self.pending[eng])
        self.pending[eng] = []
        need = {}
        for k, v in deps:
            if k == eng and (eng == "pe" or not SELF_SYNC):
                continue
            if v > need.get(k, 0):
                need[k] = v
        waits = []
        seen = self.seen[eng]
        for k, v in need.items():
            if seen.get(k, 0) < v:
                seen[k] = v
                waits.append((self.sem(k), v))
        inc = None
        ticket = None
        if fn is not None:
            if dma_res is not None:
                key = ("dma", dma_res.name)
                dma_res.dma_count += 16 * n_dma
                ticket = (key, dma_res.dma_count)
                inc = (self.sem(key), 16)
            else:
                self.count[eng] += 1
                ticket = (eng, self.count[eng])
                inc = (self.sem(eng), 1)
        self.ops[eng].append((waits, fn, inc))
        if eng == "pe" and fn is not None:
            self.pe_log.append([self.phase, getattr(fn, "n_mm", 1)])
        if ticket is not None:
            for r in reads:
                r.readers.append(ticket)
            for w in writes:
                w.last_w = ticket
                w.readers = []
        return ticket

    def barrier(self):
        for e in COMPUTE:
            if e == "pe":
                continue
            for f in COMPUTE:
                if self.count[f] > 0 and (f != e or SELF_SYNC):
                    self.pending[e].append((f, self.count[f]))

    def replay(self, eng_obj, name):
        for waits, fn, inc in self.ops[name]:
            for sem, v in waits:
                eng_obj.wait_ge(sem, v)
            if fn is None:
                continue
            r = fn(eng_obj)
            if inc is not None:
                sem, amt = inc
                if isinstance(r, (list, tuple)):
                    for i in r:
                        i.then_inc(sem, amt)
                else:
                    r.then_inc(sem, amt)


def build_program(debug=False):
    nc = bass.Bass("TRN2", target_bir_lowering=False)
    dbg = nc.dram_tensor("dbg", [8, 128, KC * TM], F32, kind="ExternalOutput").ap() if debug else None

    def din(name, shape):
        return nc.dram_tensor(name, list(shape), F32, kind="ExternalInput").ap()

    def dout(name, shape):
        return nc.dram_tensor(name, list(shape), F32, kind="ExternalOutput").ap()

    xp = din("xp", [SEQ, D])
    xs = din("xs", [128, D])
    sp_in = din("sp", [240, 512])
    sc_in = din("sc", [32, 512])
    g_mix_pre = din("g_mix_pre", [2, D])
    g_mix_post = din("g_mix_post", [2, D])
    g_ffn_pre = din("g_ffn_pre", [2, D])
    g_ffn_post = din("g_ffn_post", [2, D])
    w_in_ab = din("w_in_ab", [D, 2048])
    w_pool_grp = din("w_pool_grp", [4, 128, 128])
    pool_scale = din("pool_scale", [1, 512])
    conv_w = din("conv_w", [3, 512])
    w_out_ab = din("w_out_ab", [D, D])
    w_uv = din("w_uv", [D, 2048])
    g_v = din("g_v", [1, D])
    w_spatial = din("w_spatial", [8, 128, 128])
    b_spatial = din("b_spatial", [8, 128])
    w_out_c = din("w_out_c", [D, D])
    w_up = din("w_up", [2, D, 4096])
    w_down = din("w_down", [2, 4096, D])

    yp = dout("yp", [SEQ, D])
    ys = dout("ys", [128, D])
    o_pool_p = dout("o_pool_p", [15, 512])
    o_pool_s = dout("o_pool_s", [240, 512])
    o_conv_p = dout("o_conv_p", [2, 512])
    o_conv_s = dout("o_conv_s", [32, 512])
    o_v_s = dout("o_v_s", [128, D])

    with ExitStack() as es:
        P = Plan(nc, es)

        def sb(name, shape, dt=F32):
            return es.enter_context(nc.sbuf_tensor(name, list(shape), dt))

        Xt = sb("X", [128, KC * TM])
        Xv = Xt[:, :].rearrange("p (k t) -> p k t", t=TM)
        arena = sb("arena", [128, 23040])
        WSTt = [sb(f"wst{i}", [128, 2048]) for i in range(2)]
        WBFt = [sb(f"wbf{i}", [128, 2048], BF16) for i in range(3)]
        XTt = [sb(f"xt{i}", [128, 1024]) for i in range(3)]
        RAt = sb("ra", [128, TM])
        RBt = sb("rb", [128, TM])
        SQt = [sb(f"sq{i}", [128, 512], BF16) for i in range(4)]
        RTt = [sb(f"rt{i}", [128, 512]) for i in range(2)]
        ident = sb("ident", [128, 128])
        maskt = sb("mask", [128, 128])
        onesf = sb("onesf", [128, 128])
        onesb = sb("onesb", [128, 128], BF16)
        GT = sb("gt", [128, KC * 16])
        GTv = GT[:, :].rearrange("p (k r) -> p k r", r=16)
        GTI = sb("gti", [128, KC * 16])
        GTIv = GTI[:, :].rearrange("p (k r) -> p k r", r=16)
        WGt = sb("wg", [128, 512], BF16)
        WGv = WGt[:, :].rearrange("p (g d) -> p g d", d=128)
        WMTP = sb("wmtp", [128, 1024], BF16)
        WMTS = sb("wmts", [128, 1024], BF16)
        BIASP = sb("biasp", [128, 1024])
        GVBC = sb("gvbc", [128, 1024])
        INVC = sb("invc", [128, 64])
        INVCi = sb("invci", [128, 16], I32)
        SPT = sb("spt", [128, 4 * 240])
        SCT = sb("sct", [128, 4 * 32])
        HALOP = sb("halop", [128, 4 * 15])
        HALOC = sb("haloc", [128, 4 * 2])
        TMP16 = sb("tmp16", [128, 16])
        SS = sb("ss", [128, 16])
        RV = sb("rv", [128, 16])
        PSt = [es.enter_context(nc.psum_tensor(f"ps{i}", [128, 512], F32)) for i in range(8)]

        Hv = arena[:, 0:4608].bitcast(BF16).rearrange("p (k t) -> p k t", t=TM)
        Mv = arena[:, 4608:13824].rearrange("p (k t) -> p k t", t=TM)
        HIDv = arena[:, 13824:23040].bitcast(BF16).rearrange("p (k t) -> p k t", t=TM)
        Yv = arena[:, 13824:18432].bitcast(BF16).rearrange("p (k t) -> p k t", t=TM)
        Gv = arena[:, 18432:23040].bitcast(BF16).rearrange("p (k t) -> p k t", t=TM)
        VNv = arena[:, 13824:18432].bitcast(BF16).rearrange("p (n d) -> p n d", d=1024)
        Vv = arena[:, 4608:13824].rearrange("p (n d) -> p n d", d=1024)
        CZv = arena[:, 18432:23040].rearrange("p (k t) -> p k t", t=TM)
        Ub = [arena[:, 4608:6016], arena[:, 6016:7424]]
        S1b = arena[:, 7424:8832]
        S2b = arena[:, 8832:10240]
        Db = [arena[:, 10240:10816].bitcast(BF16), arena[:, 10816:11392].bitcast(BF16)]
        Zb = [arena[:, 11392:12580], arena[:, 12580:13768]]
        S3b = arena[:, 18432 + 2 * TM: 18432 + 2 * TM + 1408]

        NS = 3
        rX = [[Res(f"X{k}_{s}") for s in range(NS)] for k in range(KC)]
        rH = [[Res(f"H{k}_{s}") for s in range(NS)] for k in range(KC)]
        rM = [[Res(f"M{k}_{s}") for s in range(NS)] for k in range(KC)]
        rY = [[Res(f"Y{k}_{s}") for s in range(NS)] for k in range(KC)]
        rG = [[Res(f"G{k}_{s}") for s in range(NS)] for k in range(KC)]
        rSVB = [[Res(f"SVB{k}_{s}") for s in range(NS)] for k in range(KC)]
        rHID = [[Res(f"HID{k}_{s}") for s in range(NS)] for k in range(16)]
        rRA = [Res(f"RA{s}") for s in range(NS)]
        rRB = [Res(f"RB{s}") for s in range(NS)]
        rPS = [Res(f"PS{i}", excl=True) for i in range(8)]
        rWST = [Res(f"WST{i}") for i in range(2)]
        rWBF = [Res(f"WBF{i}") for i in range(3)]
        rWBFb = [Res(f"WBFb{i}") for i in range(3)]
        rXT = [Res(f"XT{i}") for i in range(3)]
        rSQ = [Res(f"SQ{i}") for i in range(4)]
        rRT = [Res(f"RT{i}") for i in range(2)]
        rU = [Res("U0"), Res("U1")]
        rS1, rS2 = Res("S1"), Res("S2")
        rD = [Res("D0"), Res("D1")]
        rZ = [Res("Z0"), Res("Z1")]
        rCZ = [Res(f"CZ{i}") for i in range(4)]
        rV = [Res(f"V{n}") for n in range(9)]
        rVN = [Res(f"VN{n}") for n in range(9)]
        rC = Res("consts")
        rID = Res("ident")
        rINV = Res("invc")
        rWG = Res("wg")
        rBIAS = Res("bias")
        rWMT = Res("wmt")
        rSPT = [Res(f"SPT{g}") for g in range(4)]
        rSCT = [Res(f"SCT{g}") for g in range(4)]
        rHALOP = [Res(f"HALOP{g}") for g in range(4)]
        rHALOC = [Res(f"HALOC{g}") for g in range(4)]
        rTMP16 = Res("TMP16")
        rSS = Res("SS")
        rSSn = [Res(f"SS{n}") for n in range(9)]

        state = {"bank": 0, "sq": 0, "rt": 0, "xt": 0}

        def next_bank():
            banks = state.get("banks", (0, 1, 2, 3, 4))
            i = state["bank"]
            state["bank"] = i + 1
            return banks[i % len(banks)]

        def next_sq():
            i = state["sq"]
            state["sq"] = (i + 1) % 4
            return i

        def next_rt():
            i = state["rt"]
            state["rt"] = (i + 1) % 2
            return i

        def next_xt():
            i = state["xt"]
            state["xt"] = (i + 1) % 3
            return i

        STAT = [5, 6, 7]
        deferred = []

        def flush_deferred(keep=0):
            nflush = len(deferred) - keep
            if nflush <= 0:
                return
            items = list(deferred[:nflush])
            del deferred[:nflush]
            for f in items:
                f()

        def op_act(out, in_, func, reads, writes, **kw):
            P.emit("act", lambda e: e.activation(out=out, in_=in_, func=func, **kw), reads, writes)

        def op_tt(eng, out, in0, in1, op, reads, writes):
            P.emit(eng, lambda e: e.tensor_tensor(out=out, in0=in0, in1=in1, op=op), reads, writes)

        def op_stt(out, in0, scalar, in1, op0, op1, reads, writes):
            P.emit("dve", lambda e: e.scalar_tensor_tensor(out=out, in0=in0, scalar=scalar, in1=in1, op0=op0, op1=op1),
                   reads, writes)

        def op_ts(eng, out, in0, scalar1, op0, reads, writes):
            P.emit(eng, lambda e: e.tensor_scalar(out=out, in0=in0, scalar1=scalar1, scalar2=None, op0=op0), reads, writes)

        def op_copy(eng, out, in_, reads, writes):
            P.emit(eng, lambda e: e.tensor_copy(out=out, in_=in_), reads, writes)

        def op_recip(out, in_, reads, writes):
            P.emit("dve", lambda e: e.reciprocal(out=out, in_=in_), reads, writes)

        def op_memset(eng, ap, val, writes):
            P.emit(eng, lambda e: e.memset(ap, val), (), writes)

        def op_mm(mms, reads, writes):
            mms = list(mms)

            def fn(e):
                last = None
                for (o, l, r, st, sp_) in mms:
                    last = e.matmul(o, lhsT=l, rhs=r, start=st, stop=sp_)
                return last
            fn.n_mm = len(mms)
            P.emit("pe", fn, reads, writes)

        def op_tr(trs, reads, writes):
            trs = list(trs)

            def fn(e):
                last = None
                for (o, i, idn) in trs:
                    last = e.transpose(o, i, idn)
                return last
            fn.n_mm = len(trs)
            P.emit("pe", fn, reads, writes)

        def op_dma(pairs, res, reads=(), writes=(), eng="sp"):
            pairs = list(pairs)
            P.emit(eng, lambda e: [e.dma_start(out=d, in_=s_) for (d, s_) in pairs], reads, writes, dma_res=res,
                   n_dma=len(pairs))

        out_res = {}

        def dma_out(dst, src, src_res):
            op_dma([(dst, src)], src_res, reads=[src_res])
            out_res[src_res.name] = src_res

        def dbg_dump(slot, subtiles):
            if not debug:
                return
            if slot >= 4:
                return
            rr = [rX[k][s_] for k in range(KC) for s_ in range(len(subtiles))]
            op_dma([(dbg[slot], Xt[:, :])], rX[0][0], reads=rr)
            out_res[rX[0][0].name] = rX[0][0]

        rDBG = Res("DBG")

        def dbg_raw(slot, lo, hi):
            if not debug:
                return
            P.barrier()
            op_copy("dve", TMP16[:, :], TMP16[:, :], [rTMP16], [rTMP16])
            op_dma([(dbg[slot][:, 0:hi - lo], arena[:, lo:hi])], rDBG, reads=[rTMP16])
            out_res[rDBG.name] = rDBG
            op_copy("dve", TMP16[:, :], TMP16[:, :], [rTMP16], [rTMP16, rDBG])
            P.barrier()

        op_memset("pool", onesf[:, :], 1.0, [rID])
        op_memset("pool", onesb[:, :], 1.0, [rID])
        P.emit("pool", lambda e: e.affine_select(out=ident[:, :], in_=onesf[:, :], pattern=[[-1, 128]],
                                                 compare_op=ALU.is_equal, fill=0.0, base=0, channel_multiplier=1),
               (), [rID])
        P.emit("pool", lambda e: e.affine_select(out=maskt[:, :], in_=onesf[:, :], pattern=[[1, 128]],
                                                 compare_op=ALU.is_ge, fill=0.0, base=0, channel_multiplier=-1),
               (), [rID])
        op_memset("pool", HALOP[:, :], 0.0, rHALOP)
        op_memset("pool", HALOC[:, :], 0.0, rHALOC)
        P.emit("pool", lambda e: e.iota(INVCi[:, :], pattern=[[1, 16]], base=1, channel_multiplier=0), (), [rINV])
        for g, w in enumerate(WINS):
            op_ts("dve", INVC[:, g * 16:(g + 1) * 16], INVCi[:, :], float(w), ALU.min, [rINV], [rINV])
        op_recip(INVC[:, :], INVC[:, :], [rINV], [rINV])

        VEC = XTt[1]
        rVEC = rXT[1]
        op_memset("pool", VEC[0:16, :], 0.0, [rVEC])
        vec_rows = [(g_mix_pre, 0), (g_mix_pre, 1), (g_mix_post, 0), (g_mix_post, 1),
                    (g_ffn_pre, 0), (g_ffn_pre, 1), (g_ffn_post, 0), (g_ffn_post, 1)]
        pairs = []
        for r, (t, l) in enumerate(vec_rows):
            pairs.append((VEC[r:r + 1, :], t[l:l + 1, :]))
        pairs.append((VEC[8:9, 0:512], pool_scale[0:1, :]))
        pairs.append((VEC[8:9, 512:1024], conv_w[0:1, :]))
        pairs.append((VEC[9:10, 0:512], conv_w[1:2, :]))
        pairs.append((VEC[9:10, 512:1024], conv_w[2:3, :]))
        op_dma(pairs, rVEC, writes=[rVEC], eng="act")
        for half in range(2):
            b = next_bank()
            op_tr([(PSt[b][:, j * 16:(j + 1) * 16], VEC[0:16, (half * 4 + j) * 128:(half * 4 + j + 1) * 128], ident[0:16, 0:16])
                   for j in range(4)], [rVEC, rID], [rPS[b]])
            op_act(GT[:, half * 64:(half + 1) * 64], PSt[b][:, 0:64], AF.Copy, [rPS[b]], [rC])
        G_MIX_PRE, G_MIX_POST, G_FFN_PRE, G_FFN_POST = 0, 2, 4, 6
        for kc in range(KC):
            op_recip(GTIv[:, kc, 0:8], GTv[:, kc, 0:8], [rC], [rC])

        def gcol(row, kc):
            return GTv[:, kc, row:row + 1]

        def pool_scale_col(g):
            return GTv[:, g, 8:9]

        def conv_col(k, i):
            if k == 0:
                return GTv[:, 4 + i, 8:9]
            if k == 1:
                return GTv[:, i, 9:10]
            return GTv[:, 4 + i, 9:10]

        BIASPv = BIASP[:, :].rearrange("p (h t) -> p h t", t=128)
        XT0v = XTt[0][:, :].rearrange("p (h s) -> p h s", s=128)
        WMTPv = WMTP[:, :].rearrange("p (h t) -> p h t", t=128)
        WMTSv = WMTS[:, :].rearrange("p (h t) -> p h t", t=128)
        SPTv = SPT[:, :].rearrange("p (g n) -> p g n", n=240)
        SCTv = SCT[:, :].rearrange("p (g n) -> p g n", n=32)

        WPG_st = WBFt[1][:, :].bitcast(F32)
        BD_st = WBFt[2][:, :].bitcast(F32)
        WSP_st = RAt[:, 0:1024]
        rWSP = [rRA[0], rRA[1], rRA[2]]
        rWPG = [rWBF[1], rWBFb[1]]
        rBD = [rWBF[2], rWBFb[2]]

        def setup_dmas(deng):
            op_dma([(WPG_st[:, 0:512].rearrange("p (g d) -> p g d", d=128), w_pool_grp.rearrange("g c d -> c g d"))],
                   rWBF[1], writes=rWPG, eng=deng)
            op_dma([(WSP_st.rearrange("p (h s) -> p h s", s=128), w_spatial.rearrange("h t s -> t h s"))],
                   rRA[0], writes=rWSP, eng=deng)
            op_memset("pool", BD_st, 0.0, rBD)
            BDv = BD_st.rearrange("p (h s) -> p h s", s=128)
            op_dma([(BDv[bb * 8:(bb + 1) * 8, :, bb * 8:(bb + 1) * 8], w_spatial[:, 0:8, 0:8].rearrange("h t s -> t h s"))
                    for bb in range(16)], rWBF[2], writes=rBD, eng=deng)
            op_dma([(BIASP[:, :], b_spatial.rearrange("a b -> (a b)").partition_broadcast(128)),
                    (GVBC[:, :], g_v.rearrange("a b -> (a b)").partition_broadcast(128))], rBIAS, writes=[rBIAS], eng=deng)

        def setup_late():
            op_copy("pool", WGt[:, :], WPG_st[:, 0:512], rWPG, [rWG])

        def setup_late_b():
            def spatial_setup(src_xt, dstT, rsrc):
                for half in range(2):
                    b = next_bank()
                    op_tr([(PSt[b][:, j * 128:(j + 1) * 128], src_xt[:, (half * 4 + j) * 128:(half * 4 + j + 1) * 128], ident[:, :])
                           for j in range(4)], rsrc + [rID], [rPS[b]])
                    for j in range(4):
                        hh = half * 4 + j
                        op_tt("dve", dstT[:, hh * 128:(hh + 1) * 128], PSt[b][:, j * 128:(j + 1) * 128], maskt[:, :], ALU.mult,
                              [rPS[b], rID], [rWMT])
            spatial_setup(WSP_st, WMTP, rWSP)
            spatial_setup(BD_st, WMTS, rBD)

        def state_dmas():
            op_dma([(XTt[0][0:120, 0:512], sp_in[0:120, :]), (XTt[0][0:32, 512:1024], sc_in[:, :])], rXT[0], writes=[rXT[0]])
            op_dma([(XTt[1][0:120, 0:512], sp_in[120:240, :])], rXT[1], writes=[rXT[1]])

        def state_compute():
            for half in range(2):
                b = next_bank()
                op_tr([(PSt[b][:, g * 128:g * 128 + 120], XTt[half][0:120, g * 128:(g + 1) * 128], ident[0:120, 0:120])
                       for g in range(4)], [rXT[half], rID], [rPS[b]])
                op_act(SPTv[:, :, half * 120:(half + 1) * 120], PSt[b][:, :].rearrange("p (g n) -> p g n", n=128)[:, :, 0:120],
                       AF.Copy, [rPS[b]], rSPT)
            b = next_bank()
            op_tr([(PSt[b][:, g * 128:g * 128 + 32], XTt[0][0:32, 512 + g * 128:512 + (g + 1) * 128], ident[0:32, 0:32]) for g in range(4)],
                  [rXT[0], rID], [rPS[b]])
            op_act(SCTv[:, :, :], PSt[b][:, :].rearrange("p (g n) -> p g n", n=128)[:, :, 0:32], AF.Copy, [rPS[b]], rSCT)

        units = []

        def colblock(Wap, cols, grow=None):
            def mk(wst):
                v = wst[:, :].rearrange("p (k n) -> p k n", n=256)
                res = []
                off = 0
                for (c0, n) in cols:
                    res.append((v[:, :, off:off + n], Wap.rearrange("(k p) n -> p k n", p=128)[:, :, c0:c0 + n]))
                    off += n
                return res
            mk.grow = grow
            return mk

        def downblock(Wap, row0, c):
            def mk(wst):
                v = wst[:, :].rearrange("p (k n) -> p k n", n=128)
                return [(v, Wap[row0:row0 + 2048, c * 128:(c + 1) * 128].rearrange("(k p) n -> p k n", p=128))]
            mk.grow = None
            return mk

        def layer_units(l):
            us = []
            if l == 0:
                us.append(colblock(w_in_ab, [(0, 256)], 0))
                for i in (0, 1):
                    us.append(colblock(w_in_ab, [(512 + 128 * i, 128), (1536 + 128 * i, 128)], 0))
                us.append(colblock(w_in_ab, [(256, 256)], 0))
                for i in (2, 3):
                    us.append(colblock(w_in_ab, [(512 + 128 * i, 128), (1536 + 128 * i, 128)], 0))
                us.append(colblock(w_in_ab, [(1024, 256)], 0))
                us.append(colblock(w_in_ab, [(1280, 256)], 0))
                for cb in range(4):
                    us.append(colblock(w_out_ab, [(cb * 256, 256)]))
            else:
                for vb in range(4):
                    us.append(colblock(w_uv, [(1024 + vb * 256, 256)], 1))
                for ub in range(4):
                    us.append(colblock(w_uv, [(ub * 256, 256)], 1))
                for cb in range(4):
                    us.append(colblock(w_out_c, [(cb * 256, 256)]))
            for half in range(2):
                for ub in range(8):
                    us.append(colblock(w_up[l], [(half * 2048 + ub * 256, 256)], 4 + l))
                for c in range(8):
                    us.append(downblock(w_down[l], half * 2048, c))
            return us

        for grp in range(2):
            for l in range(2):
                units.extend(layer_units(l))
        NU = len(units)
        ws = {"dma": 0, "conv": 0, "next": 0}

        def w_next():
            i = ws["next"]
            ws["next"] += 1
            w_advance(min(i + 4, NU - 1), min(i + 2, NU - 1))
            return [rWBF[i % 3], rWBFb[i % 3]], WBFt[i % 3]

        def w_advance(md, mc, dma_eng="sp"):
            while True:
                if ws["dma"] <= md and ws["dma"] - 2 < ws["conv"]:
                    j = ws["dma"]
                    op_dma(units[j](WSTt[j % 2]), rWST[j % 2], writes=[rWST[j % 2]], eng=dma_eng)
                    ws["dma"] += 1
                elif ws["conv"] <= mc and ws["conv"] < ws["dma"]:
                    j = ws["conv"]
                    grow = None
                    if ws.get("all_dve"):
                        op_copy("dve", WBFt[j % 3][:, 0:1024], WSTt[j % 2][:, 0:1024], [rWST[j % 2]], [rWBF[j % 3]])
                        op_copy("dve", WBFt[j % 3][:, 1024:2048], WSTt[j % 2][:, 1024:2048], [rWST[j % 2]], [rWBFb[j % 3]])
                    elif ws.get("all_act"):
                        op_act(WBFt[j % 3][:, 0:1024], WSTt[j % 2][:, 0:1024], AF.Copy, [rWST[j % 2]], [rWBF[j % 3]])
                        op_act(WBFt[j % 3][:, 1024:2048], WSTt[j % 2][:, 1024:2048], AF.Copy, [rWST[j % 2]], [rWBFb[j % 3]])
                    elif grow is None:
                        op_act(WBFt[j % 3][:, 0:1024], WSTt[j % 2][:, 0:1024], AF.Copy, [rWST[j % 2]], [rWBF[j % 3]])
                        op_copy("dve", WBFt[j % 3][:, 1024:2048], WSTt[j % 2][:, 1024:2048], [rWST[j % 2]], [rWBFb[j % 3]])
                    else:
                        for kc in range(4):
                            op_act(WBFt[j % 3][:, kc * 256:(kc + 1) * 256], WSTt[j % 2][:, kc * 256:(kc + 1) * 256], AF.Copy,
                                   [rWST[j % 2], rC], [rWBF[j % 3]], scale=gcol(grow, kc))
                        for kc in range(4, 8):
                            op_ts("dve", WBFt[j % 3][:, kc * 256:(kc + 1) * 256], WSTt[j % 2][:, kc * 256:(kc + 1) * 256],
                                  gcol(grow, kc), ALU.mult, [rWST[j % 2], rC], [rWBFb[j % 3]])
                    ws["conv"] += 1
                else:
                    break

        def rstd_chain(s, c0, n, dst_t, dst_r, sbuf_copy, ob=None):
            bk = STAT[s]
            if ob is None:
                ob = bk
            op_act(dst_t[:, c0:c0 + n], PSt[bk][:, 0:n], AF.Ln, [rPS[bk]], [dst_r], scale=1.0 / D, bias=EPS)
            op_act(PSt[ob][:, 0:n], dst_t[:, c0:c0 + n], AF.Exp, [dst_r], [rPS[ob]], scale=-0.5)
            if sbuf_copy:
                op_act(dst_t[:, c0:c0 + n], dst_t[:, c0:c0 + n], AF.Exp, [dst_r], [dst_r], scale=-0.5)

        def stat_mm(s, n, sqi, first, last):
            bk = STAT[s]
            op_mm([(PSt[bk][:, 0:n], onesb[:, :], SQt[sqi][:, 0:n], first, last)], [rSQ[sqi], rID], [rPS[bk]])

        def m_stat(c, s, c0, n, src, src_res, scale=None):
            sqi = next_sq()
            if scale is None:
                op_act(SQt[sqi][:, 0:n], src, AF.Square, src_res, [rSQ[sqi]])
            else:
                op_act(SQt[sqi][:, 0:n], src, AF.Square, src_res + [rC], [rSQ[sqi]], scale=scale)
            deferred.append(lambda: stat_mm(s, n, sqi, c == 0, c == KC - 1))

        def split_eng(kc):
            return "pool" if kc in (2, 5, 7) else "dve"

        def pre_norm_stats(s, c0, n):
            for kc in range(KC):
                sqi = next_sq()
                op_act(SQt[sqi][:, 0:n], Xv[:, kc, c0:c0 + n], AF.Square, [rX[kc][s]], [rSQ[sqi]])
                stat_mm(s, n, sqi, kc == 0, kc == KC - 1)

        def pre_norm_apply(s, c0, n, g_pre_row):
            bk = STAT[s]
            for kc in range(KC):
                op_stt(Hv[:, kc, c0:c0 + n], Xv[:, kc, c0:c0 + n], gcol(g_pre_row, kc), PSt[bk][:, 0:n], ALU.mult, ALU.mult,
                       [rX[kc][s], rPS[bk], rC], [rH[kc][s]])

        def pre_norm(s, c0, n, g_pre_row):
            pre_norm_stats(s, c0, n)
            rstd_chain(s, c0, n, RBt, rRB[s], False)
            pre_norm_apply(s, c0, n, g_pre_row)

        tstate = {"free": [0, 1], "i": 0}

        def next_tbank():
            f = tstate["free"]
            b = f[tstate["i"] % len(f)]
            tstate["i"] += 1
            return b

        def post_part(s, c0, n, do_pre, bk):
            for kc in range(KC):
                if split_eng(kc) == "pool":
                    op_tt("pool", Mv[:, kc, c0:c0 + n], Mv[:, kc, c0:c0 + n], RAt[:, c0:c0 + n], ALU.mult,
                          [rM[kc][s], rRA[s]], [rM[kc][s]])
                    op_tt("pool", Xv[:, kc, c0:c0 + n], Xv[:, kc, c0:c0 + n], Mv[:, kc, c0:c0 + n], ALU.add,
                          [rM[kc][s], rX[kc][s]], [rX[kc][s]])
                else:
                    tb = next_tbank()
                    op_tt("dve", PSt[tb][:, 0:n], Mv[:, kc, c0:c0 + n], PSt[bk][:, 0:n], ALU.mult,
                          [rM[kc][s], rPS[bk]], [rPS[tb]])
                    op_tt("dve", Xv[:, kc, c0:c0 + n], Xv[:, kc, c0:c0 + n], PSt[tb][:, 0:n], ALU.add,
                          [rPS[tb], rX[kc][s]], [rX[kc][s]])
                if do_pre:
                    sqi = next_sq()
                    op_act(SQt[sqi][:, 0:n], Xv[:, kc, c0:c0 + n], AF.Square, [rX[kc][s]], [rSQ[sqi]])
                    stat_mm(s, n, sqi, kc == 0, kc == KC - 1)

        def post_norm_residual(subtiles, g_pre_row):
            do_pre = g_pre_row is not None
            ns = len(subtiles)
            rab = []
            for s, (c0, n) in enumerate(subtiles):
                rab.append(next_bank())
                rstd_chain(s, c0, n, RAt, rRA[s], True, ob=rab[s])
            tstate["free"] = [b_ for b_ in range(5) if b_ not in rab]
            post_part(0, subtiles[0][0], subtiles[0][1], do_pre, rab[0])
            if do_pre:
                rstd_chain(0, subtiles[0][0], subtiles[0][1], RBt, rRB[0], False)
            for s in range(ns):
                if s + 1 < ns:
                    post_part(s + 1, subtiles[s + 1][0], subtiles[s + 1][1], do_pre, rab[s + 1])
                    if do_pre:
                        rstd_chain(s + 1, subtiles[s + 1][0], subtiles[s + 1][1], RBt, rRB[s + 1], False)
                if do_pre:
                    pre_norm_apply(s, subtiles[s][0], subtiles[s][1], g_pre_row)

        def proj_group(wv, cc, rw, rhs_v, rhs_r, s, c0, n, nk=KC):
            b = next_bank()
            op_mm([(PSt[b][:, 0:n], wv[:, k, cc * 128:(cc + 1) * 128], rhs_v[:, k, c0:c0 + n], k == 0, k == nk - 1)
                   for k in range(nk)], rw + [rhs_r[k][s] for k in range(nk)], [rPS[b]])
            return b

        def proj_to_M(wv, rw, rhs_v, rhs_r, subtiles, cbase, g_post_row, accumulate=False, nk=KC, ncc=2, stats=True):
            for s, (c0, n) in enumerate(subtiles):
                for cc in range(ncc):
                    c = cbase + cc
                    b = proj_group(wv, cc, rw, rhs_v, rhs_r, s, c0, n, nk)
                    flush_deferred()
                    if not accumulate:
                        op_act(Mv[:, c, c0:c0 + n], PSt[b][:, 0:n], AF.Copy, [rPS[b], rC], [rM[c][s]], scale=gcol(g_post_row, c))
                        if stats:
                            m_stat(c, s, c0, n, PSt[b][:, 0:n], [rPS[b]])
                    else:
                        op_stt(Mv[:, c, c0:c0 + n], PSt[b][:, 0:n], gcol(g_post_row, c), Mv[:, c, c0:c0 + n], ALU.mult, ALU.add,
                               [rPS[b], rM[c][s], rC], [rM[c][s]])
                        if stats:
                            m_stat(c, s, c0, n, Mv[:, c, c0:c0 + n], [rM[c][s]], scale=GTIv[:, c, g_post_row:g_post_row + 1])

        def run_group(gi):
            has_s = (gi == 1)
            state["banks"] = (0, 1, 2, 3, 4) if has_s else (0, 1, 2, 3, 4, 7)
            T = 1152 if has_s else 1024
            subtiles = [(0, 512), (512, 512)] + ([(1024, 128)] if has_s else [])
            nchunks = T // 128
            prow0 = gi * 1024
            P.barrier()

            if gi == 0:
                w_advance(0, -1, dma_eng="act")
            P.phase = f"g{gi}.load"
            for n in range(nchunks):
                xi = next_xt()
                src = xp[prow0 + n * 128: prow0 + (n + 1) * 128, :] if n < 8 else xs[:, :]
                op_dma([(XTt[xi][:, :], src)], rXT[xi], writes=[rXT[xi]])
                s = n // 4
                for half in range(2):
                    b = next_bank()
                    op_tr([(PSt[b][:, j * 128:(j + 1) * 128], XTt[xi][:, (half * 4 + j) * 128:(half * 4 + j + 1) * 128], ident[:, :])
                           for j in range(4)], [rXT[xi], rID], [rPS[b]])
                    op_act(Xv[:, half * 4:(half + 1) * 4, n * 128:(n + 1) * 128], PSt[b][:, :].rearrange("p (j t) -> p j t", t=128),
                           AF.Copy, [rPS[b]], [rX[half * 4 + j][s] for j in range(4)])
                if n == min(4 * s + 3, nchunks - 1):
                    pre_norm(s, subtiles[s][0], subtiles[s][1], G_MIX_PRE + 0)

            if gi == 0:
                setup_dmas("sp")
                w_advance(1, -1, dma_eng="sp")
            P.phase = f"g{gi}.prenorm0"
            if gi == 0:
                P.phase = "setup_late"
                setup_late()
                w_advance(3, 1)
                setup_late_b()

            for l in range(2):
                P.phase = f"g{gi}.L{l}.mixer"
                if gi == 0 and l == 1:
                    state_dmas()
                if l == 0:
                    mixer0(gi, has_s, subtiles)
                else:
                    mixer1(gi, has_s, subtiles, nchunks)
                flush_deferred()
                P.phase = f"g{gi}.L{l}.bnd_mix"
                post_norm_residual(subtiles, G_FFN_PRE + l)
                if gi == 1 and l == 0:
                    emit_state_outputs()
                dbg_dump(gi * 4 + l * 2, subtiles)
                P.barrier()
                P.phase = f"g{gi}.L{l}.ffn"
                if gi == 0 and l == 1:
                    state_compute()
                ffn(l, subtiles)
                flush_deferred()
                P.phase = f"g{gi}.L{l}.bnd_ffn"
                post_norm_residual(subtiles, (G_MIX_PRE + l + 1) if l == 0 else None)
                dbg_dump(gi * 4 + l * 2 + 1, subtiles)
                if l == 0:
                    P.barrier()

            P.phase = f"g{gi}.out"
            ob = {"i": 0}
            for n in range(nchunks):
                xi = next_xt()
                s = n // 4
                for half in range(2):
                    b = STAT[ob["i"] % 3]
                    ob["i"] += 1
                    op_tr([(PSt[b][:, j * 128:(j + 1) * 128], Xv[:, half * 4 + j, n * 128:(n + 1) * 128], ident[:, :])
                           for j in range(4)], [rX[half * 4 + j][s] for j in range(4)] + [rID], [rPS[b]])
                    op_act(XTt[xi][:, half * 512:(half + 1) * 512], PSt[b][:, :], AF.Copy, [rPS[b]], [rXT[xi]])
                dst = yp[prow0 + n * 128: prow0 + (n + 1) * 128, :] if n < 8 else ys[:, :]
                dma_out(dst, XTt[xi][:, :], rXT[xi])

        def useg(buf):
            return buf[:, 0:1039], buf[:, 1040:1408].rearrange("p (b l) -> p b l", l=23)

        def zseg(buf):
            return buf[:, 0:1026], buf[:, 1028:1188].rearrange("p (b l) -> p b l", l=10)

        def mixer0(gi, has_s, subtiles):
            Tp = 1024

            def pool_math(g):
                r = g % 2
                w = WINS[g]
                Up, Us = useg(Ub[r])
                S1p, S1s = useg(S1b)
                S2p, S2s = useg(S2b)
                S3p, S3s = useg(S3b)
                src_p, src_s, src_r = Up, Us, [rU[r]]
                if g % 2 == 0:
                    bufs = [(S1p, S1s, [rS1]), (S2p, S2s, [rS2])]
                else:
                    bufs = [(S2p, S2s, [rS2]), (S3p, S3s, [rCZ[2], rCZ[3]])]
                sh = 1
                for lev in range(g + 1):
                    dp, dsv, dr = bufs[lev % 2]
                    jmin = 2 * sh - 1
                    op_tt("pool", dp[:, jmin:1039], src_p[:, jmin:1039], src_p[:, jmin - sh:1039 - sh], ALU.add, src_r, dr)
                    if has_s:
                        op_tt("pool", dsv[:, :, jmin:23], src_s[:, :, jmin:23], src_s[:, :, jmin - sh:23 - sh], ALU.add, src_r, dr)
                    src_p, src_s, src_r = dp, dsv, dr
                    sh *= 2
                Dp = Db[r][:, 0:1024]
                Dsv = Db[r][:, 1024:1152].rearrange("p (b t) -> p b t", t=8)
                def dve_part():
                    if gi == 0:
                        op_tt("dve", TMP16[:, :], src_p[:, 15:31], INVC[:, g * 16:(g + 1) * 16], ALU.mult, src_r + [rINV], [rTMP16])
                    op_stt(Dp, src_p[:, 15:1039], 1.0 / w, Up[:, 15:1039], ALU.mult, ALU.subtract, src_r + [rU[r]], [rD[r]])
                    if gi == 0:
                        op_tt("dve", Db[r][:, 0:16], TMP16[:, :], Up[:, 15:31], ALU.subtract, [rTMP16, rU[r]], [rD[r]])
                    if has_s:
                        op_stt(Dsv, src_s[:, :, 15:23], 1.0 / w, Us[:, :, 15:23], ALU.mult, ALU.subtract, src_r + [rU[r]], [rD[r]])
                    op_copy("dve", HALOP[:, g * 15:(g + 1) * 15], Up[:, Tp:Tp + 15], [rU[r]], [rHALOP[g]])
                    if has_s:
                        op_copy("dve", SPT[:, g * 240:(g + 1) * 240].rearrange("p (b l) -> p b l", l=15), Us[:, :, 8:23],
                                [rU[r]], [rSPT[g]])

                def pe_part():
                    for s, (c0, n) in enumerate(subtiles):
                        b = next_bank()
                        op_mm([(PSt[b][:, 0:n], WGv[:, g, :], Db[r][:, c0:c0 + n], True, True)], [rD[r], rWG], [rPS[b]])
                        op_act(Yv[:, g, c0:c0 + n], PSt[b][:, 0:n], AF.Copy, [rPS[b], rC], [rY[g][s]], scale=pool_scale_col(g))
                deferred.append(lambda: (dve_part(), pe_part()))

            def conv_math(i):
                r = i % 2
                Zp, Zs = zseg(Zb[r])
                CZp = CZv[:, i, 0:1024]
                CZs = CZv[:, i, 1024:1152].rearrange("p (b t) -> p b t", t=8)

                def sl(v, a, bnd):
                    return v[:, a:bnd] if len(v.shape) == 2 else v[:, :, a:bnd]
                T1p, T1s = S1b[:, 0:1024], S1b[:, 1040:1168].rearrange("p (b t) -> p b t", t=8)
                T2p, T2s = S2b[:, 0:1024], S2b[:, 1040:1168].rearrange("p (b t) -> p b t", t=8)
                segs = [(Zp, CZp, T1p, T2p, 1024)] + ([(Zs, CZs, T1s, T2s, 8)] if has_s else [])
                for (zv, cv, t1, t2, L) in segs:
                    op_ts("dve", cv, sl(zv, 0, L), conv_col(0, i), ALU.mult, [rZ[r], rC], [rCZ[i]])
                    for k in (1, 2):
                        op_stt(cv, sl(zv, k, L + k), conv_col(k, i), cv, ALU.mult, ALU.add, [rZ[r], rCZ[i], rC], [rCZ[i]])
                op_copy("dve", HALOC[:, i * 2:(i + 1) * 2], Zp[:, 1024:1026], [rZ[r]], [rHALOC[i]])
                if has_s:
                    op_copy("dve", SCT[:, i * 32:(i + 1) * 32].rearrange("p (b l) -> p b l", l=2), Zs[:, :, 8:10], [rZ[r]], [rSCT[i]])

            def u_unit(g0):
                rw, wt = w_next()
                wv = wt[:, :].rearrange("p (k n) -> p k n", n=256)
                for cc in range(2):
                    g = g0 + cc
                    r = g % 2
                    Up, Us = useg(Ub[r])
                    op_act(Up[:, 0:15], HALOP[:, g * 15:(g + 1) * 15], AF.Copy, [rHALOP[g]], [rU[r]])
                    if has_s:
                        op_act(Us[:, :, 0:15], SPT[:, g * 240:(g + 1) * 240].rearrange("p (b l) -> p b l", l=15), AF.Copy,
                                [rSPT[g]], [rU[r]])
                    for s, (c0, n) in enumerate(subtiles):
                        b = proj_group(wv, cc, rw, Hv, rH, s, c0, n)
                        if s < 2:
                            op_act(Up[:, 15 + c0:15 + c0 + n], PSt[b][:, 0:n], AF.Copy, [rPS[b]], [rU[r]])
                        else:
                            op_act(Us[:, :, 15:23], PSt[b][:, 0:128].rearrange("p (b t) -> p b t", t=8), AF.Copy, [rPS[b]], [rU[r]])
                    pool_math(g)

            def a_unit(i):
                rw, wt = w_next()
                wv = wt[:, :].rearrange("p (k n) -> p k n", n=256)
                r = i % 2
                Zp, Zs = zseg(Zb[r])
                op_act(Zp[:, 0:2], HALOC[:, i * 2:(i + 1) * 2], AF.Copy, [rHALOC[i]], [rZ[r]])
                if has_s:
                    op_act(Zs[:, :, 0:2], SCT[:, i * 32:(i + 1) * 32].rearrange("p (b l) -> p b l", l=2), AF.Copy, [rSCT[i]], [rZ[r]])
                for s, (c0, n) in enumerate(subtiles):
                    if s < 2:
                        zdst = Zp[:, 2 + c0:2 + c0 + n]
                        psv = lambda b, n=n: PSt[b][:, 0:n]
                    else:
                        zdst = Zs[:, :, 2:10]
                        psv = lambda b: PSt[b][:, 0:128].rearrange("p (b t) -> p b t", t=8)
                    b1 = proj_group(wv, 0, rw, Hv, rH, s, c0, n)
                    op_act(zdst, psv(b1), AF.Copy, [rPS[b1]], [rZ[r]])
                    b2 = proj_group(wv, 1, rw, Hv, rH, s, c0, n)
                    op_tt("dve", zdst, psv(b2), zdst, ALU.mult, [rPS[b2], rZ[r]], [rZ[r]])
                flush_deferred()
                conv_math(i)

            def gb_unit(i0):
                rw, wt = w_next()
                wv = wt[:, :].rearrange("p (k n) -> p k n", n=256)
                for cc in range(2):
                    i = i0 + cc
                    for s, (c0, n) in enumerate(subtiles):
                        b = proj_group(wv, cc, rw, Hv, rH, s, c0, n)
                        op_tt("dve", Yv[:, 4 + i, c0:c0 + n], PSt[b][:, 0:n], CZv[:, i, c0:c0 + n], ALU.mult,
                              [rPS[b], rCZ[i]], [rY[4 + i][s]])

            u_unit(0)
            a_unit(0)
            flush_deferred()
            a_unit(1)
            u_unit(2)
            a_unit(2)
            flush_deferred()
            a_unit(3)
            gb_unit(0)
            gb_unit(2)
            flush_deferred()
            P.barrier()
            P.phase = f"g{gi}.L0.mixer.wout"
            ws["all_dve"] = True
            for cb in range(4):
                rw, wt = w_next()
                wv = wt[:, :].rearrange("p (k n) -> p k n", n=256)
                proj_to_M(wv, rw, Yv, rY, subtiles, cb * 2, G_MIX_POST + 0)
            ws["all_dve"] = False

        def emit_state_outputs():
            stg = [(XTt[0], rXT[0]), (XTt[1], rXT[1]), (XTt[2], rXT[2]), (XTt[0], rXT[0])]
            cnt = {"i": 0}

            def one(srcs, nrows, dst, reads):
                b = next_bank()
                st_t, st_r = stg[cnt["i"] % 4]
                cnt["i"] += 1
                op_tr([(PSt[b][0:nrows, g * 128:(g + 1) * 128], srcs[g], ident[:, :]) for g in range(4)], reads + [rID], [rPS[b]])
                op_act(st_t[0:nrows, 0:512], PSt[b][0:nrows, :], AF.Copy, [rPS[b]], [st_r])
                dma_out(dst, st_t[0:nrows, 0:512], st_r)
            one([HALOP[:, g * 15:(g + 1) * 15] for g in range(4)], 15, o_pool_p[:, :], rHALOP)
            one([HALOC[:, g * 2:(g + 1) * 2] for g in range(4)], 2, o_conv_p[:, :], rHALOC)
            one([SCT[:, g * 32:(g + 1) * 32] for g in range(4)], 32, o_conv_s[:, :], rSCT)
            for half in range(2):
                one([SPT[:, g * 240 + half * 120: g * 240 + (half + 1) * 120] for g in range(4)], 120,
                    o_pool_s[half * 120:(half + 1) * 120, :], rSPT)

        def mixer1(gi, has_s, subtiles, nchunks):
            for vb in range(4):
                ws["all_dve"] = True
                rw, wt = w_next()
                ws["all_dve"] = False
                wv = wt[:, :].rearrange("p (k n) -> p k n", n=256)
                for n in range(nchunks):
                    s = n // 4
                    b = next_bank()
                    op_mm([(PSt[b][:, 0:256], Hv[:, k, n * 128:(n + 1) * 128], wv[:, k, :], k == 0, k == KC - 1) for k in range(KC)],
                          rw + [rH[k][s] for k in range(KC)], [rPS[b]])
                    op_act(Vv[:, n, vb * 256:(vb + 1) * 256], PSt[b][:, 0:256], AF.Copy, [rPS[b]], [rV[n]])
                    if vb == 3:
                        op_act(VNv[:, n, :], Vv[:, n, :], AF.Square, [rV[n]], [rVN[n], rSSn[n]], accum_out=SS[:, n:n + 1])
                        op_act(RV[:, n:n + 1], SS[:, n:n + 1], AF.Ln, [rSSn[n]], [rSSn[n]], scale=1.0 / D, bias=EPS)
                        op_act(RV[:, n:n + 1], RV[:, n:n + 1], AF.Exp, [rSSn[n]], [rSSn[n]], scale=-0.5)
                        op_stt(VNv[:, n, :], Vv[:, n, :], RV[:, n:n + 1], GVBC[:, :], ALU.mult, ALU.mult,
                               [rV[n], rSSn[n], rBIAS], [rVN[n]])
                        if n == 8:
                            xi = next_xt()
                            op_stt(XTt[xi][:, :], Vv[:, n, :], RV[:, n:n + 1], GVBC[:, :], ALU.mult, ALU.mult,
                                   [rV[n], rSSn[n], rBIAS], [rXT[xi]])
                            dma_out(o_v_s[:, :], XTt[xi][:, :], rXT[xi])
            P.phase = f"g{gi}.L1.mixer.vnorm"
            P.barrier()
            if gi == 0:
                dbg_raw(5, 13824, 18432)
            for s, (c0, n) in enumerate(subtiles):
                for hh in range(KC):
                    b = next_bank()
                    if s < 2:
                        op_mm([(PSt[b][:, j * 128:(j + 1) * 128], VNv[:, s * 4 + j, hh * 128:(hh + 1) * 128], WMTPv[:, hh, :], True, True)
                               for j in range(4)], [rVN[s * 4 + j] for j in range(4)] + [rWMT], [rPS[b]])
                        bias_v = BIASPv[:, hh, :].unsqueeze(1).broadcast_to([128, 4, 128])
                        op_tt("dve", Mv[:, hh, c0:c0 + 512].rearrange("p (j t) -> p j t", t=128),
                              PSt[b][:, :].rearrange("p (j t) -> p j t", t=128), bias_v, ALU.add, [rPS[b], rBIAS], [rSVB[hh][s]])
                    else:
                        op_mm([(PSt[b][:, 0:128], VNv[:, 8, hh * 128:(hh + 1) * 128], WMTSv[:, hh, :], True, True)],
                              [rVN[8], rWMT], [rPS[b]])
                        op_tt("dve", Mv[:, hh, c0:c0 + 128].rearrange("p (b t) -> p b t", t=8),
                              PSt[b][:, 0:128].rearrange("p (b t) -> p b t", t=8),
                              BIASPv[:, hh, 0:8].unsqueeze(1).broadcast_to([128, 16, 8]), ALU.add, [rPS[b], rBIAS], [rSVB[hh][s]])
            if gi == 0:
                dbg_raw(6, 4608, 13824)
            P.phase = f"g{gi}.L1.mixer.u"
            for ub in range(4):
                ws["all_act"] = (ub < 3)
                rw, wt = w_next()
                ws["all_act"] = False
                wv = wt[:, :].rearrange("p (k n) -> p k n", n=256)
                for s, (c0, n) in enumerate(subtiles):
                    for cc in range(2):
                        c = ub * 2 + cc
                        b = proj_group(wv, cc, rw, Hv, rH, s, c0, n)
                        op_tt("dve", Gv[:, c, c0:c0 + n], PSt[b][:, 0:n], Mv[:, c, c0:c0 + n], ALU.mult,
                              [rPS[b], rSVB[c][s]], [rG[c][s]])
            P.barrier()
            if gi == 0:
                dbg_raw(7, 18432, 23040)
            ws["all_dve"] = True
            for cb in range(4):
                rw, wt = w_next()
                wv = wt[:, :].rearrange("p (k n) -> p k n", n=256)
                proj_to_M(wv, rw, Gv, rG, subtiles, cb * 2, G_MIX_POST + 1)
            ws["all_dve"] = False

        def ffn(l, subtiles):
            for half in range(2):
                for ub in range(8):
                    rw, wt = w_next()
                    wv = wt[:, :].rearrange("p (k n) -> p k n", n=256)
                    for s, (c0, n) in enumerate(subtiles):
                        for cc in range(2):
                            j = ub * 2 + cc
                            b = proj_group(wv, cc, rw, Hv, rH, s, c0, n)
                            ri = next_rt()
                            op_act(RTt[ri][:, 0:n], PSt[b][:, 0:n], AF.Relu, [rPS[b]], [rRT[ri]])
                            op_tt("pool", HIDv[:, j, c0:c0 + n], RTt[ri][:, 0:n], RTt[ri][:, 0:n], ALU.mult, [rRT[ri]], [rHID[j][s]])
                for c in range(8):
                    rw, wt = w_next()
                    wv = wt[:, :].rearrange("p (k n) -> p k n", n=128)
                    proj_to_M(wv, rw, HIDv, rHID, subtiles, c, G_FFN_POST + l, accumulate=(half == 1), nk=16, ncc=1, stats=(half == 1))

        run_group(0)
        run_group(1)
        flush_deferred()
        P.emit("sp", None, (), list(out_res.values()))

        import os as _os
        if _os.environ.get("PE_LOG"):
            import json as _json
            _json.dump(P.pe_log, open(_os.environ["PE_LOG"], "w"))
        with nc.Block() as block:
            @block.sync
            def _(e):
                P.replay(e, "sp")

            @block.tensor
            def _(e):
                P.replay(e, "pe")

            @block.scalar
            def _(e):
                P.replay(e, "act")

            @block.vector
            def _(e):
                P.replay(e, "dve")

            @block.gpsimd
            def _(e):
                P.replay(e, "pool")
    return nc


_CACHE = {}


def kernel(x_prompt, x_sample, state_pool, state_conv,
           g_mix_pre, g_mix_post, g_ffn_pre, g_ffn_post,
           w_in_ab, w_pool_grp, pool_scale, conv_w, w_out_ab,
           w_uv, g_v, w_spatial, b_spatial, w_out_c, w_up, w_down):
    f = lambda a: np.ascontiguousarray(np.asarray(a, dtype=np.float32))
    if "nc" not in _CACHE:
        _CACHE["nc"] = build_program()
    nc = _CACHE["nc"]
    shared = {
        "g_mix_pre": f(g_mix_pre), "g_mix_post": f(g_mix_post), "g_ffn_pre": f(g_ffn_pre), "g_ffn_post": f(g_ffn_post),
        "w_in_ab": f(w_in_ab[0]), "w_pool_grp": f(w_pool_grp[0]), "pool_scale": f(pool_scale), "conv_w": f(conv_w[0]),
        "w_out_ab": f(w_out_ab[0]), "w_uv": f(w_uv[0]), "g_v": f(g_v), "w_spatial": f(w_spatial[0]),
        "b_spatial": f(b_spatial[0]), "w_out_c": f(w_out_c[0]), "w_up": f(w_up), "w_down": f(w_down),
    }
    xp = f(x_prompt)
    xsm = f(x_sample)
    spl = f(state_pool)
    scv = f(state_conv)
    in_maps = []
    for c in range(NCORES):
        m = dict(shared)
        m["xp"] = xp[c]
        m["xs"] = xsm[c * DEC_B:(c + 1) * DEC_B].reshape(128, D)
        m["sp"] = spl[0, c * DEC_B:(c + 1) * DEC_B].reshape(240, 512)
        m["sc"] = scv[0, c * DEC_B:(c + 1) * DEC_B].reshape(32, 512)
        in_maps.append(m)
    res = run_bass_kernel_spmd(nc, in_maps, core_ids=list(range(NCORES)))
    R = res.results
    y_prompt = np.stack([R[c]["yp"] for c in range(NCORES)], axis=0)
    y_sample = np.concatenate([R[c]["ys"].reshape(DEC_B, DEC_T, D) for c in range(NCORES)], axis=0)
    pool_p = np.stack([R[c]["o_pool_p"] for c in range(NCORES)], axis=0)[None]
    pool_s = np.concatenate([R[c]["o_pool_s"].reshape(DEC_B, 15, 512) for c in range(NCORES)], axis=0)[None]
    conv_p = np.stack([R[c]["o_conv_p"] for c in range(NCORES)], axis=0)[None]
    conv_s = np.concatenate([R[c]["o_conv_s"].reshape(DEC_B, 2, 512) for c in range(NCORES)], axis=0)[None]
    v_s = np.concatenate([R[c]["o_v_s"].reshape(DEC_B, DEC_T, D) for c in range(NCORES)], axis=0)[None]
    return (y_prompt.astype(np.float32), y_sample.astype(np.float32), pool_p.astype(np.float32), pool_s.astype(np.float32),
            conv_p.astype(np.float32), conv_s.astype(np.float32), v_s.astype(np.float32))
```
